# Optimizing a Trainium2 kernel written in Bass

```python
import math
import jax, jax.numpy as jnp
from jax import lax
import numpy as np

D_MODEL = 1024
BATCH = 8
SEQ = 2048
DEPTH = 1
DEC_BATCH = 128
DEC_SEQ = 1
PAST_LEN = 16384
PAGE_SIZE = 128

A_HEADS = 4
A_DK = 128
A_DV = 128
A_KWIDTH = A_HEADS * A_DK
A_WIDTH = A_HEADS * A_DV
CONV_WIDTH = 4
CONV_CH = 2 * A_KWIDTH + A_WIDTH
DELTA_CHUNK = 64
B_GROUPS = 4
B_WIDTH = D_MODEL // 2
B_CHUNK = 128
MEM_LEN = 256
C_HEADS = 4
C_DH = 128
C_WIDTH = C_HEADS * C_DH
N_BRANCH = 3
EPS = 1e-6
SPLIT_SIZES = (A_KWIDTH, A_KWIDTH, A_WIDTH, A_HEADS, A_HEADS, A_WIDTH,
               B_WIDTH, B_WIDTH, B_WIDTH, C_WIDTH, C_WIDTH, N_BRANCH * D_MODEL)
IN_COLS = sum(SPLIT_SIZES)

kernel_name = "hybrid_delta_chunkmlp_memxattn_step"


def _split_points():
    pts, acc = [], 0
    for s in SPLIT_SIZES[:-1]:
        acc += s
        pts.append(acc)
    return pts


def rmsnorm(x, g):
    xf = x.astype(jnp.float32)
    y = xf * lax.rsqrt(jnp.mean(xf * xf, axis=-1, keepdims=True) + EPS)
    return (y * g.astype(jnp.float32)).astype(x.dtype)


def layernorm(x, g, b):
    xf = x.astype(jnp.float32)
    mu = jnp.mean(xf, axis=-1, keepdims=True)
    xc = xf - mu
    y = xc * lax.rsqrt(jnp.mean(xc * xc, axis=-1, keepdims=True) + EPS)
    return (y * g.astype(jnp.float32) + b.astype(jnp.float32)).astype(x.dtype)


def l2norm(x):
    return x * lax.rsqrt(jnp.sum(x * x, axis=-1, keepdims=True) + EPS)


def causal_conv(buf, x, w):
    T = x.shape[1]
    xp = jnp.concatenate([buf.astype(x.dtype), x], axis=1)
    y = xp[:, 0:T] * w[0]
    for j in range(1, CONV_WIDTH):
        y = y + xp[:, j:j + T] * w[j]
    return jax.nn.silu(y), xp[:, -(CONV_WIDTH - 1):]


def gated_delta_chunked(q, k, v, beta, g, S0):
    B, T, H, _ = q.shape
    C = DELTA_CHUNK
    Tp = -(-T // C) * C
    N = Tp // C
    pad = Tp - T

    def padt(a):
        return jnp.pad(a, [(0, 0), (0, pad)] + [(0, 0)] * (a.ndim - 2))

    def chunk4(a):
        return padt(a).reshape(B, N, C, H, a.shape[-1]).transpose(1, 0, 3, 2, 4)

    def chunk3(a):
        return padt(a).reshape(B, N, C, H).transpose(1, 0, 3, 2)

    q, k, v = chunk4(q), chunk4(k), chunk4(v)
    beta, g = chunk3(beta), chunk3(g)
    gc = jnp.cumsum(g, axis=-1)
    idx = jnp.arange(C)
    causal = idx[:, None] >= idx[None, :]
    strict = idx[:, None] > idx[None, :]
    decay = jnp.exp(jnp.where(causal, gc[..., :, None] - gc[..., None, :], -jnp.inf))
    k_beta = k * beta[..., None]
    L = jnp.where(strict, jnp.einsum('nbhid,nbhjd->nbhij', k_beta, k) * decay, 0.0)
    eye = jnp.eye(C, dtype=q.dtype)
    Tinv = lax.linalg.triangular_solve(eye + L, jnp.broadcast_to(eye, L.shape),
                                       left_side=True, lower=True)
    u = jnp.einsum('nbhij,nbhjv->nbhiv', Tinv, v * beta[..., None])
    w = jnp.einsum('nbhij,nbhjk->nbhik', Tinv, k_beta * jnp.exp(gc)[..., None])
    attn = jnp.where(causal, jnp.einsum('nbhid,nbhjd->nbhij', q, k) * decay, 0.0)
    q_dec = q * jnp.exp(gc)[..., None]
    k_dec = k * jnp.exp(gc[..., -1:] - gc)[..., None]
    g_last = jnp.exp(gc[..., -1])

    def step(S, xs):
        u_i, w_i, attn_i, qd_i, kd_i, gl_i = xs
        v_new = u_i - jnp.einsum('bhck,bhkv->bhcv', w_i, S)
        o = jnp.einsum('bhck,bhkv->bhcv', qd_i, S) + jnp.einsum('bhij,bhjv->bhiv', attn_i, v_new)
        S = S * gl_i[..., None, None] + jnp.einsum('bhck,bhcv->bhkv', kd_i, v_new)
        return S, o

    S, o = lax.scan(step, S0, (u, w, attn, q_dec, k_dec, g_last))
    o = o.transpose(1, 0, 3, 2, 4).reshape(B, Tp, H, v.shape[-1])[:, :T]
    return o, S


def delta_branch(q, k, v, beta_logit, alpha, gate, S0, conv_buf, conv_w, a_log, dt_bias, a_norm_g):
    B, T, _ = q.shape
    f32 = jnp.float32
    qkv, new_buf = causal_conv(conv_buf, jnp.concatenate([q, k, v], axis=-1), conv_w)
    q, k, v = jnp.split(qkv, [A_KWIDTH, 2 * A_KWIDTH], axis=-1)
    q = l2norm(q.reshape(B, T, A_HEADS, A_DK).astype(f32)) * (A_DK ** -0.5)
    k = l2norm(k.reshape(B, T, A_HEADS, A_DK).astype(f32))
    v = v.reshape(B, T, A_HEADS, A_DV).astype(f32)
    beta = jax.nn.sigmoid(beta_logit.astype(f32))
    g = -jnp.exp(a_log.astype(f32)) * jax.nn.softplus(alpha.astype(f32) + dt_bias.astype(f32))
    o, S = gated_delta_chunked(q, k, v, beta, g, S0.astype(f32))
    o = rmsnorm(o, a_norm_g) * jax.nn.silu(gate.reshape(B, T, A_HEADS, A_DV).astype(f32))
    return o.reshape(B, T, A_WIDTH).astype(gate.dtype), S.astype(S0.dtype), new_buf


def chunk_mlp_branch(u, v, gate, ln_v_g, ln_v_b, w_spatial, b_spatial):
    B, T, _ = u.shape
    vn = layernorm(v, ln_v_g, ln_v_b)
    Tp = -(-T // B_CHUNK) * B_CHUNK
    N = Tp // B_CHUNK
    vp = jnp.pad(vn, ((0, 0), (0, Tp - T), (0, 0))).reshape(B, N, B_CHUNK, B_GROUPS, B_WIDTH // B_GROUPS)
    idx = jnp.arange(B_CHUNK)
    ws = jnp.where(idx[:, None] >= idx[None, :], w_spatial, 0.0)
    s = jnp.einsum('gts,bnsgc->bntgc', ws, vp) + b_spatial.T[None, None, :, :, None]
    s = s.reshape(B, Tp, B_WIDTH)[:, :T]
    return u * s * jax.nn.silu(gate), vn


def memory_kv(mem, mem_norm_g, w_mem_kv):
    B, M, _ = mem.shape
    kv = rmsnorm(mem, mem_norm_g) @ w_mem_kv
    k, v = jnp.split(kv, [C_WIDTH], axis=-1)
    return k.reshape(B, M, C_HEADS, C_DH), v.reshape(B, M, C_HEADS, C_DH)


def memory_branch(q, gate, mem_k, mem_v):
    B, T, _ = q.shape
    f32 = jnp.float32
    qh = q.reshape(B, T, C_HEADS, C_DH).astype(f32)
    s = jnp.einsum('bthd,bmhd->bhtm', qh, mem_k.astype(f32)) * (C_DH ** -0.5)
    p = jax.nn.softmax(s, axis=-1)
    o = jnp.einsum('bhtm,bmhd->bthd', p, mem_v.astype(f32)).reshape(B, T, C_WIDTH)
    return (o * jax.nn.silu(gate.astype(f32))).astype(q.dtype)


def mixer_layer(x, mem_k, mem_v, S0, conv_buf, norm_g, w_in, conv_w, a_log, dt_bias, a_norm_g,
                ln_v_g, ln_v_b, w_spatial, b_spatial, w_br_a, w_br_b, w_br_c, b_gate, w_out):
    B, T, _ = x.shape
    h = rmsnorm(x, norm_g)
    proj = h @ w_in
    (aq, ak, av, abeta, aalpha, agate, bu, bv, bgate, cq, cgate, mg) = jnp.split(proj, _split_points(), axis=-1)
    ya, S, buf = delta_branch(aq, ak, av, abeta, aalpha, agate, S0, conv_buf, conv_w, a_log, dt_bias, a_norm_g)
    yb, vn = chunk_mlp_branch(bu, bv, bgate, ln_v_g, ln_v_b, w_spatial, b_spatial)
    yc = memory_branch(cq, cgate, mem_k, mem_v)
    gates = jax.nn.sigmoid((mg.reshape(B, T, N_BRANCH, D_MODEL) + b_gate).astype(jnp.float32)).astype(x.dtype)
    merged = gates[:, :, 0] * (ya @ w_br_a) + gates[:, :, 1] * (yb @ w_br_b) + gates[:, :, 2] * (yc @ w_br_c)
    return x + merged @ w_out, S, buf, vn


def setup_inputs(seed: int = 0) -> dict:
    key = jax.random.key(seed)
    ks = list(jax.random.split(key, 32))
    f32 = jnp.float32

    def nrm(i, shape, s=1.0):
        return jax.random.normal(ks[i], shape, f32) * s

    dt = jnp.exp(jax.random.uniform(ks[10], (DEPTH, A_HEADS), f32)
                 * (math.log(0.1) - math.log(0.001)) + math.log(0.001))
    return {
        "x_prompt": nrm(0, (BATCH, SEQ, D_MODEL)),
        "x_sample": nrm(1, (DEC_BATCH, DEC_SEQ, D_MODEL)),
        "cache_mem_k": nrm(2, (DEPTH, DEC_BATCH, MEM_LEN, C_HEADS, C_DH)),
        "cache_mem_v": nrm(3, (DEPTH, DEC_BATCH, MEM_LEN, C_HEADS, C_DH)),
        "state_delta": nrm(4, (DEPTH, DEC_BATCH, A_HEADS, A_DK, A_DV), 0.1),
        "state_conv": nrm(5, (DEPTH, DEC_BATCH, CONV_WIDTH - 1, CONV_CH)),
        "mem_prompt": nrm(6, (BATCH, MEM_LEN, D_MODEL)),
        "norm_g": 1.0 + nrm(7, (DEPTH, D_MODEL), 0.02),
        "w_in": nrm(8, (DEPTH, D_MODEL, IN_COLS), D_MODEL ** -0.5),
        "conv_w": nrm(9, (DEPTH, CONV_WIDTH, CONV_CH), 0.5),
        "a_log": jnp.log(jax.random.uniform(ks[11], (DEPTH, A_HEADS), f32, 1.0, 16.0)),
        "dt_bias": dt + jnp.log(-jnp.expm1(-dt)),
        "a_norm_g": 1.0 + nrm(12, (DEPTH, A_DV), 0.02),
        "ln_v_g": 1.0 + nrm(13, (DEPTH, B_WIDTH), 0.02),
        "ln_v_b": nrm(14, (DEPTH, B_WIDTH), 0.02),
        "w_spatial": nrm(15, (DEPTH, B_GROUPS, B_CHUNK, B_CHUNK), 0.5 * B_CHUNK ** -0.5),
        "b_spatial": 1.0 + nrm(16, (DEPTH, B_GROUPS, B_CHUNK), 0.02),
        "mem_norm_g": 1.0 + nrm(17, (DEPTH, D_MODEL), 0.02),
        "w_mem_kv": nrm(18, (DEPTH, D_MODEL, 2 * C_WIDTH), D_MODEL ** -0.5),
        "w_br_a": nrm(19, (DEPTH, A_WIDTH, D_MODEL), A_WIDTH ** -0.5),
        "w_br_b": nrm(20, (DEPTH, B_WIDTH, D_MODEL), B_WIDTH ** -0.5),
        "w_br_c": nrm(21, (DEPTH, C_WIDTH, D_MODEL), C_WIDTH ** -0.5),
        "b_gate": nrm(22, (DEPTH, N_BRANCH, D_MODEL), 0.02),
        "w_out": nrm(23, (DEPTH, D_MODEL, D_MODEL), D_MODEL ** -0.5),
        "final_norm_g": 1.0 + nrm(24, (D_MODEL,), 0.02),
    }


def reference(x_prompt, x_sample, cache_mem_k, cache_mem_v, state_delta, state_conv, mem_prompt,
              norm_g, w_in, conv_w, a_log, dt_bias, a_norm_g, ln_v_g, ln_v_b, w_spatial, b_spatial,
              mem_norm_g, w_mem_kv, w_br_a, w_br_b, w_br_c, b_gate, w_out, final_norm_g):
    hp, hs = x_prompt, x_sample
    bp = x_prompt.shape[0]
    sd_p, sc_p, mk_p, mv_p, sd_s, sc_s, cv_s = [], [], [], [], [], [], []
    for l in range(DEPTH):
        lw = (norm_g[l], w_in[l], conv_w[l], a_log[l], dt_bias[l], a_norm_g[l], ln_v_g[l], ln_v_b[l],
              w_spatial[l], b_spatial[l], w_br_a[l], w_br_b[l], w_br_c[l], b_gate[l], w_out[l])
        mk, mv = memory_kv(mem_prompt, mem_norm_g[l], w_mem_kv[l])
        s0 = jnp.zeros((bp, A_HEADS, A_DK, A_DV), x_prompt.dtype)
        c0 = jnp.zeros((bp, CONV_WIDTH - 1, CONV_CH), x_prompt.dtype)
        hp, s_p, c_p, _ = mixer_layer(hp, mk, mv, s0, c0, *lw)
        hs, s_s, c_s, v_s = mixer_layer(hs, cache_mem_k[l], cache_mem_v[l], state_delta[l], state_conv[l], *lw)
        sd_p.append(s_p)
        sc_p.append(c_p)
        mk_p.append(mk)
        mv_p.append(mv)
        sd_s.append(s_s)
        sc_s.append(c_s)
        cv_s.append(v_s)
    y_prompt = rmsnorm(hp, final_norm_g)
    y_sample = rmsnorm(hs, final_norm_g)
    return (y_prompt, y_sample, jnp.stack(sd_p), jnp.stack(sc_p), jnp.stack(mk_p), jnp.stack(mv_p),
            jnp.stack(sd_s), jnp.stack(sc_s), jnp.stack(cv_s))
```

```python
import contextlib
import os
import sys
import numpy as np
import concourse.bass as bass
import concourse.mybir as mybir
from concourse.bass_utils import run_bass_kernel_spmd

F32 = mybir.dt.float32
BF16 = mybir.dt.bfloat16
AF = mybir.ActivationFunctionType
ALU = mybir.AluOpType
AX = mybir.AxisListType

ENGS = ("pe", "dve", "act", "pool", "sp")
N_DMA_SEMS = 12
EPS = 1e-6
T = 2048
NT = 16
TS = 16
OFF = dict(aq=0, ak=512, av=1024, ab=1536, ag=1544, bu=2056, bv=2568, bg=3080, cq=3592, cg=4104, mg=4616)
BIG = 30000.0


class Sched:
    def __init__(self, nc):
        self.nc = nc
        self.ops = []
        self.last_write = {}
        self.readers = {}
        self.clock = {e: {} for e in ENGS}
        self.cnt = {e: 0 for e in ENGS}
        self.last_compute = {}
        self.dma_i = {e: 0 for e in ENGS}
        self.dma_last = {}
        self.per_eng = {e: [] for e in ENGS}
        self.pending_bar = {e: [] for e in ENGS}
        self.ever = set()

    def _add(self, eng, fn, reads, writes, is_dma):
        op = dict(id=len(self.ops), eng=eng, fn=fn, dma=is_dma, waits=[])
        fr = sys._getframe(2)
        op["where"] = []
        while fr is not None and len(op["where"]) < 4:
            op["where"].append(fr.f_lineno)
            fr = fr.f_back
        deps = []
        for b in reads:
            w = self.last_write.get(b)
            if w is not None:
                deps.append((w, "raw"))
            elif isinstance(b, str) and b.startswith("hT") and b not in self.ever:
                raise RuntimeError("read of %s recorded before its producer" % b)
        for b in writes:
            w = self.last_write.get(b)
            if w is not None:
                deps.append((w, "waw"))
            for r in self.readers.get(b, ()):
                deps.append((r, "war"))
        if self.pending_bar[eng]:
            for d in self.pending_bar[eng]:
                deps.append((d, "bar"))
            self.pending_bar[eng] = []
        if is_dma:
            i = self.dma_i[eng]
            self.dma_i[eng] += 1
            slot = i % N_DMA_SEMS
            prev = self.dma_last.get((eng, slot))
            if prev is not None:
                deps.append((prev, "slot"))
            op["sem"] = ("dma", eng, slot)
            op["val"] = 16 * (i // N_DMA_SEMS + 1)
            op["inc"] = 16
            self.dma_last[(eng, slot)] = op
        else:
            self.cnt[eng] += 1
            op["sem"] = ("eng", eng)
            op["val"] = self.cnt[eng]
            op["inc"] = 1
            self.last_compute[eng] = op
        clk = self.clock[eng]
        need = {}
        used = []
        for d, kind in deps:
            if (not d["dma"]) and (not is_dma) and d["eng"] == eng:
                if kind != "raw" or eng == "pe":
                    continue
            k, v = d["sem"], d["val"]
            if clk.get(k, 0) >= v:
                continue
            used.append(d)
            if need.get(k, 0) < v:
                need[k] = v
        if not os.environ.get("MK_NOSNAP"):
            for d in used:
                for kk, vv in d["snap"].items():
                    if clk.get(kk, 0) < vv:
                        clk[kk] = vv
        for k, v in need.items():
            if clk.get(k, 0) < v:
                clk[k] = v
            op["waits"].append((k, v))
        snap = dict(clk)
        if not is_dma:
            snap[op["sem"]] = op["val"]
        op["snap"] = snap
        for b in reads:
            self.readers.setdefault(b, []).append(op)
        for b in writes:
            self.last_write[b] = op
            self.readers[b] = []
            self.ever.add(b)
        self.ops.append(op)
        self.per_eng[eng].append(op)
        return op

    def op(self, eng, fn, reads=(), writes=()):
        pr = [k for k in reads if isinstance(k, str) and k.startswith("pb")]
        if pr:
            reads = [k for k in reads if k not in pr]
            writes = list(writes) + [k for k in pr if k not in writes]
        return self._add(eng, fn, tuple(reads), tuple(writes), False)

    def dma(self, eng, out, in_, reads=(), writes=(), **kw):
        def fn(e, out=out, in_=in_, kw=kw):
            return e.dma_start(out=out, in_=in_, **kw)
        return self._add(eng, fn, tuple(reads), tuple(writes), True)

    def barrier(self):
        outstanding = list(self.last_compute.values()) + list(self.dma_last.values())
        for e in ENGS:
            self.pending_bar[e] = list(outstanding)
        self.last_write = {}
        self.readers = {}

    def emit(self):
        nc = self.nc
        if os.environ.get("MK_VERBOSE"):
            print("SCHED counts", self.cnt, "dma", self.dma_i, flush=True)
        with contextlib.ExitStack() as st:
            sems = {}
            for e in ENGS:
                sems[("eng", e)] = st.enter_context(nc.semaphore("s_" + e))
                for s in range(N_DMA_SEMS):
                    if self.dma_i[e] > s:
                        sems[("dma", e, s)] = st.enter_context(nc.semaphore("d_%s_%d" % (e, s)))
            block = st.enter_context(nc.Block())
            hooks = dict(pe=block.tensor, dve=block.vector, act=block.scalar, pool=block.gpsimd, sp=block.sync)

            def make(e):
                ops = self.per_eng[e]

                def body(eng):
                    for op in ops:
                        for k, v in op["waits"]:
                            eng.wait_ge(sems[k], v)
                        try:
                            inst = op["fn"](eng)
                        except Exception:
                            print("FAILED OP recorded at lines", op["where"], "engine", e)
                            raise
                        inst.then_inc(sems[op["sem"]], op["inc"])
                    for s in range(N_DMA_SEMS):
                        last = self.dma_last.get((e, s))
                        if last is not None:
                            eng.wait_ge(sems[last["sem"]], last["val"])
                return body

            for e in ENGS:
                if self.per_eng[e]:
                    hooks[e](make(e))


def _prod(s):
    r = 1
    for v in s:
        r *= v
    return r


class Arena:
    def __init__(self, ap, total):
        self.ap, self.total, self.off = ap, total, 0
        self.top = total

    def at_f32(self, off, *shape):
        n = _prod(shape)
        assert off + n <= self.total
        return self._shape(self.ap[:, off:off + n], shape)

    def bf16_top(self, *shape):
        n = _prod(shape)
        w = (n + 3) // 4 * 2
        self.top -= w
        assert self.off <= self.top
        a = self.ap[:, self.top:self.top + w].bitcast(BF16)[:, 0:n]
        return self._shape(a, shape)

    def _shape(self, a, shape):
        if len(shape) == 1:
            return a
        if len(shape) == 2:
            return a.rearrange("p (a b) -> p a b", a=shape[0])
        if len(shape) == 3:
            return a.rearrange("p (a b c) -> p a b c", a=shape[0], b=shape[1])
        raise ValueError

    def f32(self, *shape):
        n = _prod(shape)
        n2 = (n + 1) // 2 * 2
        a = self.ap[:, self.off:self.off + n]
        self.off += n2
        assert self.off <= self.top, ("arena overflow", self.off, self.top)
        return self._shape(a, shape)

    def bf16(self, *shape):
        n = _prod(shape)
        w = (n + 3) // 4 * 2
        a = self.ap[:, self.off:self.off + w].bitcast(BF16)[:, 0:n]
        self.off += w
        assert self.off <= self.top, ("arena overflow", self.off, self.top)
        return self._shape(a, shape)


def build():
    nc = bass.Bass("TRN2", target_bir_lowering=False)

    def din(name, shape):
        return nc.dram_tensor(name, list(shape), F32, kind="ExternalInput").ap()

    def dout(name, shape):
        return nc.dram_tensor(name, list(shape), F32, kind="ExternalOutput").ap()

    x_d = din("x", (T, 1024))
    xs_d = din("xs", (TS, 1024))
    cmk_d = din("cmk", (TS, 256, 512))
    cmv_d = din("cmv", (TS, 256, 512))
    sd_d = din("sd", (TS, 4, 128, 128))
    sc_d = din("sc", (TS, 3, 1536))
    mem_d = din("mem", (256, 1024))
    norm_g_d = din("norm_g", (1024,))
    w_in_d = din("w_in", (1024, 7688))
    conv_w_d = din("conv_w", (4, 1536))
    a_log_d = din("a_log", (4,))
    dt_bias_d = din("dt_bias", (4,))
    a_norm_g_d = din("a_norm_g", (128,))
    ln_v_g_d = din("ln_v_g", (512,))
    ln_v_b_d = din("ln_v_b", (512,))
    w_sp_d = din("w_spatial", (4, 128, 128))
    b_sp_d = din("b_spatial", (4, 128))
    mem_norm_g_d = din("mem_norm_g", (1024,))
    w_mkv_d = din("w_mem_kv", (1024, 1024))
    w_bra_d = din("w_br_a", (512, 1024))
    w_brb_d = din("w_br_b", (512, 1024))
    w_brc_d = din("w_br_c", (512, 1024))
    b_gate_d = din("b_gate", (3, 1024))
    w_out_d = din("w_out", (1024, 1024))
    fng_d = din("final_norm_g", (1024,))

    y_d = dout("y", (T, 1024))
    ys_d = dout("ys", (TS, 1024))
    sdp_d = dout("sdp", (4, 128, 128))
    scp_d = dout("scp", (3, 1536))
    mkp_d = dout("mkp", (256, 512))
    mvp_d = dout("mvp", (256, 512))
    sds_d = dout("sds", (TS, 4, 128, 128))
    scs_d = dout("scs", (TS, 3, 1536))
    cvs_d = dout("cvs", (TS, 512))

    S = Sched(nc)
    AW = 49000
    STOP = float(os.environ.get("MK_STOP", "99"))
    with contextlib.ExitStack() as st:
        arena_t = st.enter_context(nc.sbuf_tensor("arena", [128, AW], F32))
        ps_t = st.enter_context(nc.psum_tensor("ps", [128, 4096], F32))
        A = Arena(arena_t[:, :], AW)

        def bank(i):
            return ps_t[:, i * 512:(i + 1) * 512]

        def bkeys(i, q0=0, q1=4):
            return ["pb%d" % i]

        rr = {"b": 0, "pool": list(range(8))}

        def nextbank():
            rr["b"] = (rr["b"] + 1) % len(rr["pool"])
            return rr["pool"][rr["b"]]

        def mm(out, lhsT, rhs, start, stop, reads, writes):
            S.op("pe", lambda e: e.matmul(out, lhsT=lhsT, rhs=rhs, start=start, stop=stop), reads, writes)

        def tr(out, in_, ident, reads, writes):
            S.op("pe", lambda e: e.transpose(out=out, in_=in_, identity=ident), reads, writes)

        def act(out, in_, func, reads, writes, **kw):
            S.op("act", lambda e: e.activation(out=out, in_=in_, func=func, **kw), reads, writes)

        def tt(eng, out, in0, in1, op, reads, writes):
            S.op(eng, lambda e: e.tensor_tensor(out=out, in0=in0, in1=in1, op=op), reads, writes)

        def ts(eng, out, in0, s1, s2, op0, op1, reads, writes):
            if op1 is None:
                S.op(eng, lambda e: e.tensor_scalar(out=out, in0=in0, scalar1=s1, scalar2=None, op0=op0), reads, writes)
            else:
                S.op(eng, lambda e: e.tensor_scalar(out=out, in0=in0, scalar1=s1, scalar2=s2, op0=op0, op1=op1), reads, writes)

        def stt(out, in0, scalar, in1, op0, op1, reads, writes, accum_out=None):
            S.op("dve", lambda e: e.scalar_tensor_tensor(out=out, in0=in0, scalar=scalar, in1=in1, op0=op0, op1=op1,
                                                         accum_out=accum_out), reads, writes)

        def cp(eng, out, in_, reads, writes):
            if eng == "act":
                S.op("act", lambda e: e.copy(out=out, in_=in_), reads, writes)
            else:
                S.op(eng, lambda e: e.tensor_copy(out=out, in_=in_), reads, writes)

        def memset(eng, ap, val, writes):
            S.op(eng, lambda e: e.memset(ap, val), (), writes)

        def asel(out, in_, pattern, cmp_op, fill, base, cm, reads, writes):
            S.op("pool", lambda e: e.affine_select(out=out, in_=in_, pattern=pattern, compare_op=cmp_op, fill=fill,
                                                   base=base, channel_multiplier=cm), reads, writes)

        def bc(ap, shape):
            return ap.broadcast_to(list(shape))

        def custom(base_ap, extra_off, dims):
            return bass.AP(tensor=base_ap.tensor, offset=base_ap.offset + extra_off, ap=[list(base_ap.ap[0])] + dims)

        evq = {"i": 0}

        def evac_eng():
            evq["i"] += 1
            return "act" if evq["i"] % 2 else "dve"

        ident_f = A.f32(128)
        ident_b = A.bf16(128)
        ones_f = A.f32(128)
        ones_b = A.bf16(128)
        U_f = A.f32(128)
        NEG_up = A.f32(128)
        POS_lo = A.f32(128)
        neghalf = A.f32(2)
        BD32 = A.bf16(128)
        M32o = A.bf16(128)
        M64o = A.bf16(128)
        gT = A.f32(8)
        mgT = A.f32(8)
        cwT = A.f32(12, 4)
        bgT = A.f32(3, 8)
        ang = A.f32(2)
        alB = A.f32(4)
        dtB = A.f32(4)
        stats = A.f32(20, 4)
        hT = A.bf16(8, 2068)
        yaT = A.bf16(4, 2064)
        ycT = A.bf16_top(4, 2064)
        ybT = A.bf16_top(4, 2064)
        TOP_YC = AW - (4 * 2064 + 3) // 4 * 2
        TOP_YB = A.top
        A.top = AW
        xn19 = A.f32(1536)
        ags = A.f32(512)
        ba = A.f32(16, 8)
        bas = A.f32(8)
        sc64 = {}
        for nm in ("beta", "nbeta", "g", "gc", "ngc", "gl", "egl", "ekd", "bge", "tmpa", "tmpb"):
            sc64[nm] = A.f32(64)
        sS = {nm: A.f32(4) for nm in ("beta", "g", "tmpa", "tmpb")}
        PERSIST = A.off

        memset("pool", ident_f, 0.0, ["ident_f"])
        asel(ident_f, ident_f, [[-1, 128]], ALU.not_equal, 1.0, 0, 1, ["ident_f"], ["ident_f"])
        cp("dve", ident_b, ident_f, ["ident_f"], ["ident_b"])
        memset("pool", ones_f, 1.0, ["ones_f"])
        memset("pool", ones_b, 1.0, ["ones_b"])
        memset("pool", U_f, 1.0, ["U_f"])
        asel(U_f, U_f, [[1, 128]], ALU.is_ge, 0.0, 0, -1, ["U_f"], ["U_f"])
        memset("pool", NEG_up, 0.0, ["NEG_up"])
        asel(NEG_up, NEG_up, [[1, 128]], ALU.is_ge, -BIG, 0, -1, ["NEG_up"], ["NEG_up"])
        memset("pool", POS_lo, 0.0, ["POS_lo"])
        asel(POS_lo, POS_lo, [[-1, 128]], ALU.is_gt, BIG, 0, 1, ["POS_lo"], ["POS_lo"])
        memset("pool", neghalf, -0.5, ["neghalf"])
        memset("pool", BD32, 0.0, ["BD32"])
        memset("pool", M32o, 0.0, ["M32o"])
        memset("pool", M64o, 0.0, ["M64o"])
        for q_ in range(2):
            S.op("pool", lambda e, q_=q_: e.memset(M32o[64 * q_:64 * q_ + 64, 64 * q_:64 * q_ + 64], 1.0), ["M32o"], ["M32o"])
        for q_ in range(4):
            S.op("pool", lambda e, q_=q_: e.memset(BD32[32 * q_:32 * q_ + 32, 32 * q_:32 * q_ + 32], 1.0), ["BD32"], ["BD32"])
            S.op("pool", lambda e, q_=q_: e.memset(M32o[32 * q_:32 * q_ + 32, 32 * q_:32 * q_ + 32], 0.0), ["M32o"], ["M32o"])
        S.op("pool", lambda e: e.memset(M64o[64:128, 0:64], 1.0), ["M64o"], ["M64o"])
        memset("pool", stats, 0.0, ["stats"])
        cst_g = A.at_f32(40000, 128)
        cst_m = A.at_f32(40128, 128)
        cst_c = A.at_f32(40256, 4, 128)
        cst_b = A.at_f32(40768, 3, 128)
        S.dma("sp", cst_g[0:8, :], norm_g_d.rearrange("(k p) -> k p", p=128), writes=["cst_g"])
        S.dma("sp", cst_m[0:8, :], mem_norm_g_d.rearrange("(k p) -> k p", p=128), writes=["cst_m"])
        S.dma("sp", cst_c[0:12], conv_w_d.rearrange("j (s p) -> s j p", p=128), writes=["cst_c"])
        S.dma("sp", cst_b[0:8], b_gate_d.rearrange("i (k p) -> k i p", p=128), writes=["cst_b"])
        bi = nextbank()
        tr(bank(bi)[:, 0:8], cst_g[0:8, :], ident_f[0:8, 0:8], ["cst_g", "ident_f"], bkeys(bi))
        tr(bank(bi)[:, 8:16], cst_m[0:8, :], ident_f[0:8, 0:8], ["cst_m", "ident_f"], bkeys(bi))
        for j in range(4):
            tr(bank(bi)[:, 16 + 12 * j:28 + 12 * j], cst_c[0:12, j, :], ident_f[0:12, 0:12], ["cst_c", "ident_f"], bkeys(bi))
        for i in range(3):
            tr(bank(bi)[:, 64 + 8 * i:72 + 8 * i], cst_b[0:8, i, :], ident_f[0:8, 0:8], ["cst_b", "ident_f"], bkeys(bi))
        cp("dve", gT, bank(bi)[:, 0:8], bkeys(bi), ["gT"])
        cp("dve", mgT, bank(bi)[:, 8:16], bkeys(bi), ["mgT"])
        cp("dve", cwT.rearrange("p s j -> p j s"), bank(bi)[:, 16:64].rearrange("p (j s) -> p j s", j=4), bkeys(bi), ["cwT"])
        cp("dve", bgT, bank(bi)[:, 64:88].rearrange("p (i k) -> p i k", i=3), bkeys(bi), ["bgT"])
        S.dma("sp", ang[:, 0:1], a_norm_g_d.rearrange("(p o) -> p o", o=1), writes=["ang"], allow_slow_non_contiguous=True)
        S.dma("sp", alB, a_log_d.partition_broadcast(128), writes=["alB"])
        S.dma("sp", dtB, dt_bias_d.partition_broadcast(128), writes=["dtB"])

        def rsqrt_col(dst, src, rows, scale, rk, wk):
            ts("dve", dst, src, scale, EPS, ALU.mult, ALU.add, rk, wk)
            tt("pool", dst, dst, neghalf[0:rows, 0:1], ALU.pow, wk + ["neghalf"], wk)

        def norm_A(src_rows, rows, gcol, dst, dcol, dkey, idx, xt, xs_bf, junk):
            b = idx % len(xt)
            S.dma("sp", xt[b][0:rows, :], src_rows, writes=["xt%d" % b])
            ss = stats[0:rows, idx, 0:1]
            rs = stats[0:rows, idx, 1:2]
            sk = "st%d" % idx
            stt(junk[0:rows, :], xt[b][0:rows, :], 1.0, xt[b][0:rows, :], ALU.mult, ALU.mult,
                ["xt%d" % b, "stats"], ["junk", sk], accum_out=ss)
            rsqrt_col(rs, ss, rows, 1.0 / 1024, [sk], [sk + "r"])
            act(xs_bf[b][0:rows, :], xt[b][0:rows, :], AF.Copy, ["xt%d" % b, sk + "r"], ["xs%d" % b], scale=rs)
            bi = nextbank()
            pT = bank(bi).bitcast(BF16)
            for k in range(8):
                tr(pT[:, k * 128:k * 128 + rows], xs_bf[b][0:rows, k * 128:(k + 1) * 128], ident_b[0:rows, 0:rows],
                   ["xs%d" % b, "ident_b"], bkeys(bi))
            return (bi, pT, rows, gcol, dst, dcol, dkey)

        def norm_B(ctx):
            bi, pT, rows, gcol, dst, dcol, dkey = ctx
            tt("dve", dst[:, :, dcol:dcol + rows],
               pT.rearrange("p (k t) -> p k t", k=8)[:, :, 0:rows],
               bc(gcol.unsqueeze(2), [128, 8, rows]), ALU.mult, bkeys(bi) + ["gT", "mgT"], [dkey])

        def norm_T(*a):
            norm_B(norm_A(*a))

        mark0 = A.off
        A.off = AW - 7000
        xt = [A.f32(1024) for _ in range(4)]
        xs_bf = [A.bf16(1024) for _ in range(4)]
        junk = A.bf16(1024)
        def stage0_gen():
            prev = None
            for n in range(NT + 1):
                if n < NT:
                    ctx = norm_A(x_d[n * 128:(n + 1) * 128, :], 128, gT, hT, n * 128, "hT%d" % n, n, xt, xs_bf, junk)
                else:
                    ctx = norm_A(xs_d, TS, gT, hT, 2048, "hT16", 16, xt, xs_bf, junk)
                if prev is not None:
                    norm_B(prev)
                    if n == NT:
                        cp("dve", hT[:, :, 2064:2067], hT[:, :, 2045:2048], ["hT15"], ["hT16b"])
                prev = ctx
                yield
            norm_B(prev)
            yield

        HS = ["hT16", "hT16b"]

        def hkeys(tb):
            return ["hT%d" % (4 * tb + i) for i in range(4)]

        def load_w(dst, src, key):
            S.dma("pool", dst, src.rearrange("(k p) n -> p k n", p=128), writes=[key])

        def proj_fm(wt, c0, c1, t0, t1, out_ap, rkeys, wkeys, K=8, src=None):
            src = hT if src is None else src
            for k in range(K):
                mm(out_ap, wt[:, k, c0:c1], src[:, k, t0:t1], k == 0, k == K - 1, rkeys, wkeys)

        def proj_tm(wt, c0, c1, t0, t1, out_ap, rkeys, wkeys, K=8, src=None):
            src = hT if src is None else src
            for k in range(K):
                mm(out_ap, src[:, k, t0:t1], wt[:, k, c0:c1], k == 0, k == K - 1, rkeys, wkeys)

        if STOP <= 0:
            S.emit()
            return nc
        A.off = mark0
        wsm = [A.bf16(8, 128) for _ in range(4)]
        w8 = A.bf16(8, 8)
        slab_base = A.off
        slab_pre = [A.f32(2052), A.f32(2052)]
        slab_post = [A.f32(2048), A.f32(2048)]
        slab_end = A.off
        S_all = A.at_f32(slab_base, 64, 128)
        SLABK = ["pre%d_z" % i for i in range(2)] + ["pre%d_%d" % (i, t) for i in range(2) for t in range(4)] + \
                ["post%d_%d" % (i, t) for i in range(2) for t in range(4)]
        sq_bs = [A.bf16(2048), A.bf16(2048)]
        rn4 = [[A.f32(512), A.f32(512)], [A.f32(512), A.f32(512)]]
        sq_b = sq_bs[0]
        rn = rn4[0]
        QT = A.bf16(2048)
        KT = A.bf16(2048)
        VT = A.bf16(2048)
        sgT2 = [A.bf16(2048), A.bf16(2048)]
        kvn = A.bf16(16, 256)
        kd_n = A.bf16(16, 128)
        qdT = A.bf16(16, 128)
        attnT = A.bf16(16, 128)
        NKwT = A.bf16(16, 128)
        u_b = A.bf16(16, 128)
        o_sb = A.f32(2048)
        S_f = A.f32(128)
        S_b2 = [A.bf16(128), A.bf16(128)]
        wu_bf = [A.bf16(256) for _ in range(4)]
        NFL = 4
        dgc = [A.f32(128) for _ in range(NFL)]
        Dup = [A.f32(128) for _ in range(NFL)]
        Dlo = [A.f32(128) for _ in range(NFL)]
        egB = [A.bf16(128) for _ in range(NFL)]
        AA = [[A.bf16(256) for _ in range(2)] for _ in range(NFL)]
        PT = [A.bf16(128) for _ in range(NFL)]
        sB = {nm: sc64[nm] for nm in sc64}

        memset("dve", slab_pre[0][:, 0:3], 0.0, ["pre0_z"])
        memset("dve", slab_pre[1][:, 0:3], 0.0, ["pre1_z"])

        wi = {"i": 0}
        SLAB_BANKS = [5, 6, 7]
        REC_WS = [0, 1]
        REC_SU = [2, 3]
        REC_OT = 4
        sbk = {"i": 0}

        def slab_bank():
            sbk["i"] += 1
            return SLAB_BANKS[sbk["i"] % 3]

        def next_wsm(c0):
            w = wsm[wi["i"] % 4]
            k = "wsm%d" % (wi["i"] % 4)
            wi["i"] += 1
            load_w(w, w_in_d[:, c0:c0 + 128], k)
            return w, k

        slab_ctr = {"i": 0}

        def slab_gen(h, parts, si):
            pre, post = slab_pre[si], slab_post[si]
            sqb, rns = sq_bs[si], rn4[si]
            for part in parts:
                if part == 3:
                    w, wk = next_wsm(OFF["ag"] + h * 128)
                    for tb in range(4):
                        bi = slab_bank()
                        proj_fm(w, 0, 128, tb * 512, (tb + 1) * 512, bank(bi), [wk] + hkeys(tb), bkeys(bi))
                        yield
                        act(sgT2[h % 2][:, tb * 512:(tb + 1) * 512], bank(bi), AF.Silu, bkeys(bi), ["sgT%d_%d" % (h % 2, tb)])
                        yield
                    bi = slab_bank()
                    proj_tm(w, 0, 128, 2048, 2064, bank(bi)[0:TS, 0:128], [wk] + HS, bkeys(bi))
                    yield
                    cp("dve", ags[0:TS, h * 128:(h + 1) * 128], bank(bi)[0:TS, 0:128], bkeys(bi), ["ags%d" % h])
                    yield
                    continue
                c0 = part * 512 + h * 128
                w, wk = next_wsm(c0)
                sl = part * 4 + h
                dst = [QT, KT, VT][part]
                dk_ = ["QT", "KT", "VT"][part]
                for tb in range(4):
                    tsl = slice(tb * 512, (tb + 1) * 512)
                    bi = slab_bank()
                    proj_fm(w, 0, 128, tb * 512, (tb + 1) * 512, bank(bi), [wk] + hkeys(tb), bkeys(bi))
                    yield
                    cp("act", pre[:, 3 + tb * 512:3 + (tb + 1) * 512], bank(bi), bkeys(bi), ["pre%d_%d" % (si, tb)])
                    prk = ["pre%d_%d" % (si, tb), "pre%d_z" % si] + (["pre%d_%d" % (si, tb - 1)] if tb else [])
                    pk_ = "post%d_%d" % (si, tb)
                    act(post[:, tsl], pre[:, tb * 512:tb * 512 + 512], AF.Copy, prk + ["cwT"], [pk_], scale=cwT[:, sl, 0:1])
                    yield
                    for j in range(1, 4):
                        stt(post[:, tsl], pre[:, tb * 512 + j:tb * 512 + j + 512], cwT[:, sl, j:j + 1], post[:, tsl],
                            ALU.mult, ALU.add, prk + ["cwT", pk_], [pk_])
                    yield
                    if part == 2:
                        act(VT[:, tsl], post[:, tsl], AF.Silu, [pk_], ["VT%d" % tb])
                        yield
                    else:
                        act(post[:, tsl], post[:, tsl], AF.Silu, [pk_], [pk_])
                        tt("pool", sqb[:, tsl], post[:, tsl], post[:, tsl], ALU.mult, [pk_], ["sq_b%d_%d" % (si, tb)])
                        yield
                        b2_ = slab_bank()
                        mm(bank(b2_), ones_b, sqb[:, tsl], True, True, ["ones_b", "sq_b%d_%d" % (si, tb)], bkeys(b2_))
                        r = rns[tb % 2]
                        rk = "rn%d_%d" % (si, tb % 2)
                        act(r, bank(b2_), AF.Ln, bkeys(b2_), [rk], bias=EPS)
                        act(r, r, AF.Exp, [rk], [rk], scale=-0.5)
                        yield
                        stt(dst[:, tsl], post[:, tsl], (128.0 ** -0.5) if part == 0 else 1.0, r, ALU.mult, ALU.mult,
                            [pk_, rk], ["%s%d" % (dk_, tb)])
                        yield
                bi = slab_bank()
                proj_tm(w, 0, 128, 2048, 2067, bank(bi)[0:19, 0:128], [wk] + HS, bkeys(bi))
                yield
                cp("dve", xn19[0:19, c0:c0 + 128], bank(bi)[0:19, 0:128], bkeys(bi), ["xn19_%d_%d" % (part, h)])
                yield

        QK4 = lambda nm: ["%s%d" % (nm, t) for t in range(4)]

        def kv_transposes(h):
            beta_h = sB["beta"].rearrange("p (n h) -> p n h", h=4)[:, :, h]
            bge_h = sB["bge"].rearrange("p (n h) -> p n h", h=4)[:, :, h]
            ekd_h = sB["ekd"].rearrange("p (n h) -> p n h", h=4)[:, :, h]
            for half in range(2):
                hs = slice(half * 8, (half + 1) * 8)
                bi = nextbank()
                pT = bank(bi).bitcast(BF16)
                for j in range(8):
                    n = half * 8 + j
                    tr(pT[:, j * 128:(j + 1) * 128], KT[:, n * 128:(n + 1) * 128], ident_b, QK4("KT") + ["ident_b"], bkeys(bi))
                pv = pT.rearrange("p (n d) -> p n d", n=8)
                tt("dve", kvn[:, hs, 0:128], pv, bc(bge_h[:, hs].unsqueeze(2), [128, 8, 128]), ALU.mult, bkeys(bi) + ["sc_bge"], ["kvn_k"])
                tt("dve", kd_n[:, hs, :], pv, bc(ekd_h[:, hs].unsqueeze(2), [128, 8, 128]), ALU.mult, bkeys(bi) + ["sc_ekd"], ["kd_n"])
                bi = nextbank()
                pT = bank(bi).bitcast(BF16)
                for j in range(8):
                    n = half * 8 + j
                    tr(pT[:, j * 128:(j + 1) * 128], VT[:, n * 128:(n + 1) * 128], ident_b, QK4("VT") + ["ident_b"], bkeys(bi))
                pv = pT.rearrange("p (n d) -> p n d", n=8)
                tt("dve", kvn[:, hs, 128:256], pv, bc(beta_h[:, hs].unsqueeze(2), [128, 8, 128]), ALU.mult, bkeys(bi) + ["sc_beta"], ["kvn_v"])

        def precompute(h):
            for g0 in range(0, NT, NFL):
                units = [(n, n - g0) for n in range(g0, g0 + NFL)]
                bA = lambda f: 2 * f
                bB = lambda f: 2 * f + 1
                qA = lambda f, i: bank(bA(f))[:, i * 128:(i + 1) * 128]
                kA = lambda f: bkeys(bA(f))
                kB = lambda f: bkeys(bB(f))
                for n, f in units:
                    c = n * 4 + h
                    tsl = slice(n * 128, (n + 1) * 128)
                    ts("pool", dgc[f], ident_f, sB["gc"][:, c:c + 1], None, ALU.mult, None, ["ident_f", "sc_gc"], ["dgc%d" % f])
                    mm(qA(f, 0), ones_f, dgc[f], True, True, ["ones_f", "dgc%d" % f], kA(f))
                    mm(qA(f, 1), KT[:, tsl], KT[:, tsl], True, True, QK4("KT"), kA(f))
                    mm(qA(f, 2), KT[:, tsl], QT[:, tsl], True, True, QK4("KT") + QK4("QT"), kA(f))
                for n, f in units:
                    c = n * 4 + h
                    gcc = sB["gc"][:, c:c + 1]
                    stt(Dup[f], qA(f, 0), gcc, NEG_up, ALU.subtract, ALU.add, kA(f) + ["sc_gc", "NEG_up"], ["Dup%d" % f])
                    stt(Dlo[f], qA(f, 0), gcc, POS_lo, ALU.subtract, ALU.add, kA(f) + ["sc_gc", "POS_lo"], ["Dlo%d" % f])
                    act(egB[f], qA(f, 0), AF.Exp, kA(f), ["egB%d" % f])
                    act(Dup[f], Dup[f], AF.Exp, ["Dup%d" % f], ["Dup%d" % f])
                    act(Dlo[f], Dlo[f], AF.Exp, ["Dlo%d" % f], ["Dlo%d" % f], scale=-1.0)
                for n, f in units:
                    c = n * 4 + h
                    tsl = slice(n * 128, (n + 1) * 128)
                    tt("pool", qdT[:, n, :], QT[:, tsl], egB[f], ALU.mult, QK4("QT") + ["egB%d" % f], ["qdT%d" % n])
                    stt(AA[f][0][:, 0:128], qA(f, 1), sB["nbeta"][:, c:c + 1], Dlo[f], ALU.mult, ALU.mult,
                        kA(f) + ["sc_nbeta", "Dlo%d" % f], ["AA%d_0" % f])
                    tt("dve", attnT[:, n, :], qA(f, 2), Dup[f], ALU.mult, kA(f) + ["Dup%d" % f], ["attnT%d" % n])
                for n, f in units:
                    q3b = bank(bB(f)).bitcast(BF16)[:, 0:128]
                    tr(q3b, AA[f][0][:, 0:128], ident_b, ["AA%d_0" % f, "ident_b"], kB(f))
                Mo32T = lambda f: Dup[f].bitcast(BF16)[:, 0:128]
                Mo64 = lambda f: Dup[f].bitcast(BF16)[:, 128:256]
                P_ = lambda f: Dlo[f].bitcast(BF16)[:, 0:128]
                U_ = lambda f: Dlo[f].bitcast(BF16)[:, 128:256]
                for n, f in units:
                    q3b = bank(bB(f)).bitcast(BF16)[:, 0:128]
                    cp("act", egB[f], q3b, kB(f) + ["qdT%d" % n], ["egB%d" % f])
                    tt("dve", AA[f][0][:, 128:256], q3b, BD32, ALU.mult, kB(f) + ["BD32"], ["AA%d_0T" % f])
                    tt("dve", PT[f], AA[f][0][:, 128:256], ident_b, ALU.add, ["AA%d_0T" % f, "ident_b"], ["PT%d" % f])
                for n, f in units:
                    tt("pool", Mo64(f), AA[f][0][:, 0:128], M64o, ALU.mult, ["AA%d_0" % f, "M64o", "attnT%d" % n], ["Dup%d" % f])
                    tt("pool", AA[f][0][:, 0:128], AA[f][0][:, 0:128], BD32, ALU.mult, ["AA%d_0" % f, "BD32", "Dup%d" % f], ["AA%d_0" % f])
                for n, f in units:
                    tt("pool", Mo32T(f), egB[f], M32o, ALU.mult, ["egB%d" % f, "M32o", "Dup%d" % f], ["Dup%d" % f])
                for lvl in range(1, 5):
                    sk_ = lambda f: "AA%d_%d" % (f, (lvl - 1) % 2)
                    dk2 = lambda f: "AA%d_%d" % (f, lvl % 2)
                    for n, f in units:
                        src_ = AA[f][(lvl - 1) % 2]
                        mm(qA(f, 0), src_[:, 128:256], src_[:, 0:128], True, True, [sk_(f), sk_(f) + "T"], kA(f))
                        if lvl < 4:
                            mm(qA(f, 1), src_[:, 0:128], src_[:, 128:256], True, True, [sk_(f), sk_(f) + "T"], kA(f))
                    for n, f in units:
                        dst_ = AA[f][lvl % 2]
                        if lvl < 4:
                            cp(evac_eng(), dst_, bank(bA(f))[:, 0:256], kA(f), [dk2(f), dk2(f) + "T"])
                        else:
                            cp(evac_eng(), dst_[:, 0:128], qA(f, 0), kA(f), [dk2(f)])
                    for n, f in units:
                        dst_ = AA[f][lvl % 2]
                        mm(bank(bB(f))[:, 0:128], ident_b, PT[f], True, False, ["ident_b", "PT%d" % f], kB(f))
                        mm(bank(bB(f))[:, 0:128], dst_[:, 0:128], PT[f], False, True, [dk2(f), "PT%d" % f], kB(f))
                    for n, f in units:
                        cp(evac_eng(), PT[f], bank(bB(f))[:, 0:128], kB(f), ["PT%d" % f])
                for n, f in units:
                    trb = bank(bB(f)).bitcast(BF16)[:, 0:128]
                    tr(trb, PT[f], ident_b, ["PT%d" % f, "ident_b"], kB(f))
                for n, f in units:
                    trb = bank(bB(f)).bitcast(BF16)[:, 0:128]
                    cp(evac_eng(), P_(f), trb, kB(f) + ["AA%d_0" % f], ["Dlo%d" % f])
                for n, f in units:
                    mm(qA(f, 0), Mo32T(f), P_(f), True, True, ["Dup%d" % f, "Dlo%d" % f], kA(f))
                for n, f in units:
                    cp(evac_eng(), U_(f), qA(f, 0), kA(f), ["Dlo%d" % f])
                for n, f in units:
                    mm(bank(bB(f))[:, 0:128], ident_b, P_(f), True, False, ["ident_b", "Dlo%d" % f], kB(f))
                    mm(bank(bB(f))[:, 0:128], PT[f], U_(f), False, True, ["PT%d" % f, "Dlo%d" % f], kB(f))
                for n, f in units:
                    cp(evac_eng(), P_(f), bank(bB(f))[:, 0:128], kB(f), ["Dlo%d" % f])
                for n, f in units:
                    trb = bank(bA(f)).bitcast(BF16)[:, 0:128]
                    tr(trb, P_(f), ident_b, ["Dlo%d" % f, "ident_b"], kA(f))
                for n, f in units:
                    trb = bank(bA(f)).bitcast(BF16)[:, 0:128]
                    cp(evac_eng(), PT[f], trb, kA(f), ["PT%d" % f])
                for n, f in units:
                    mm(bank(bB(f))[:, 0:128], Mo64(f), PT[f], True, True, ["Dup%d" % f, "PT%d" % f], kB(f))
                for n, f in units:
                    cp(evac_eng(), U_(f), bank(bB(f))[:, 0:128], kB(f), ["Dlo%d" % f])
                for n, f in units:
                    mm(qA(f, 0), ident_b, PT[f], True, False, ["ident_b", "PT%d" % f], kA(f))
                    mm(qA(f, 0), P_(f), U_(f), False, True, ["Dlo%d" % f], kA(f))
                for n, f in units:
                    cp(evac_eng(), PT[f], qA(f, 0), kA(f), ["PT%d" % f])
                for n, f in units:
                    mm(bank(bA(f))[:, 0:256], PT[f], kvn[:, n, :], True, True, ["kvn_k", "kvn_v", "PT%d" % f], kA(f))
                for n, f in units:
                    cp("act", wu_bf[f][:, 0:128], bank(bA(f))[:, 0:128], kA(f), ["wu%d" % f])
                    cp("dve", u_b[:, n, :], bank(bA(f))[:, 128:256], kA(f), ["u%d" % n])
                for n, f in units:
                    mm(bank(bB(f))[:, 0:128], wu_bf[f][:, 0:128], kd_n[:, n, :], True, True, ["wu%d" % f, "kd_n"], kB(f))
                    mm(bank(bB(f))[:, 128:256], wu_bf[f][:, 0:128], attnT[:, n, :], True, True, ["wu%d" % f, "attnT%d" % n], kB(f))
                for n, f in units:
                    act(NKwT[:, n, :], bank(bB(f))[:, 0:128], AF.Copy, kB(f), ["NKwT%d" % n], scale=-1.0)
                    tt("dve", qdT[:, n, :], qdT[:, n, :], bank(bB(f))[:, 128:256], ALU.subtract, kB(f) + ["qdT%d" % n], ["qdT%d" % n])

        def recurrence_gen(h):
            memset("dve", S_f, 0.0, ["S_f"])
            for n in range(NT):
                c = n * 4 + h
                b2, b3 = REC_SU[n % 2], REC_OT
                Sp, Spk = S_b2[(n + 1) % 2], "S_b%d" % ((n + 1) % 2)
                Sn, Snk = S_b2[n % 2], "S_b%d" % (n % 2)
                mm(bank(b2)[:, 0:128], kd_n[:, n, :], u_b[:, n, :], True, n == 0, ["kd_n", "u%d" % n], bkeys(b2))
                if n > 0:
                    mm(bank(b2)[:, 0:128], NKwT[:, n, :], Sp, False, True, ["NKwT%d" % n, Spk], bkeys(b2))
                mm(bank(b3)[:, 0:128], u_b[:, n, :], attnT[:, n, :], True, n == 0, ["u%d" % n, "attnT%d" % n], bkeys(b3))
                if n > 0:
                    mm(bank(b3)[:, 0:128], Sp, qdT[:, n, :], False, True, [Spk, "qdT%d" % n], bkeys(b3))
                egl_c = sB["egl"][:, c:c + 1]
                stt(Sn, S_f, egl_c, bank(b2)[:, 0:128], ALU.mult, ALU.add, ["S_f", "sc_egl"] + bkeys(b2), [Snk])
                stt(S_f, S_f, egl_c, bank(b2)[:, 0:128], ALU.mult, ALU.add, ["S_f", "sc_egl"] + bkeys(b2), ["S_f"])
                cp("act", o_sb[:, n * 128:(n + 1) * 128], bank(b3)[:, 0:128], bkeys(b3), ["o_sb%d" % (n // 4)])
                yield
            S.dma("sp", sdp_d[h], S_f, reads=["S_f"])

        def outnorm(h):
            for tb in range(4):
                tsl = slice(tb * 512, (tb + 1) * 512)
                tt("pool", sq_b[:, tsl], o_sb[:, tsl], o_sb[:, tsl], ALU.mult, ["o_sb%d" % tb], ["sq_b0_%d" % tb])
                bi = slab_bank()
                mm(bank(bi), ones_b, sq_b[:, tsl], True, True, ["ones_b", "sq_b0_%d" % tb], bkeys(bi))
                r = rn[tb % 2]
                rk = "rn0_%d" % (tb % 2)
                act(r, bank(bi), AF.Ln, bkeys(bi), [rk], bias=EPS, scale=1.0 / 128)
                act(r, r, AF.Exp, [rk], [rk], scale=-0.5)
                stt(r, o_sb[:, tsl], ang[:, 0:1], r, ALU.mult, ALU.mult, ["o_sb%d" % tb, "ang", rk], [rk])
                tt("dve", yaT[:, h, tsl], r, sgT2[h % 2][:, tsl], ALU.mult, [rk, "sgT%d_%d" % (h % 2, tb)], ["yaT"])

        def drain(*gens, weights=None):
            gens = [g for g in gens if g is not None]
            wts = {id(g): (weights[i] if weights else 1) for i, g in enumerate(gens)}
            while gens:
                for g in list(gens):
                    for _ in range(wts[id(g)]):
                        try:
                            next(g)
                        except StopIteration:
                            gens.remove(g)
                            break

        g0 = stage0_gen()
        rr["pool"] = [0, 1, 2, 3, 4]
        for _ in range(int(os.environ.get("MK_PRE", "7"))):
            next(g0)
        drain(g0, slab_gen(0, [0, 2], 0), slab_gen(0, [1, 3], 1), weights=[1, 2, 2])
        rr["pool"] = list(range(8))
        load_w(w8, w_in_d[:, OFF["ab"]:OFF["ab"] + 8], "w8")
        bi = nextbank()
        for n in range(NT):
            proj_tm(w8, 0, 8, n * 128, (n + 1) * 128, bank(bi)[:, n * 8:(n + 1) * 8], ["w8", "hT%d" % n], bkeys(bi, 0, 1))
        cp("dve", ba.rearrange("p n c -> p (n c)"), bank(bi)[:, 0:128], bkeys(bi, 0, 1), ["ba"])
        bi = nextbank()
        proj_tm(w8, 0, 8, 2048, 2064, bank(bi)[0:TS, 0:8], ["w8"] + HS, bkeys(bi, 0, 1))
        cp("dve", bas[0:TS, :], bank(bi)[0:TS, 0:8], bkeys(bi, 0, 1), ["bas"])

        def scalars(rows, logits, nn, out, px=""):
            v = lambda ap: ap[0:rows, 0:nn * 4].rearrange("p (n h) -> p n h", h=4)
            beta, g_, ta, tb_ = v(out["beta"]), v(out["g"]), v(out["tmpa"]), v(out["tmpb"])
            logits = logits[0:rows]
            act(beta, logits[:, :, 0:4], AF.Sigmoid, ["ba", "bas"], [px + "sc_beta"])
            tt("dve", ta, logits[:, :, 4:8], bc(dtB[0:rows, :].unsqueeze(1), [rows, nn, 4]), ALU.add,
               ["ba", "bas", "dtB"], [px + "sc_ta"])
            stt(tb_, ta, -1.0, ta, ALU.mult, ALU.max, [px + "sc_ta"], [px + "sc_tb"])
            act(tb_, tb_, AF.Exp, [px + "sc_tb"], [px + "sc_tb"], scale=-1.0)
            act(tb_, tb_, AF.Ln, [px + "sc_tb"], [px + "sc_tb"], bias=1.0)
            stt(ta, ta, 0.0, tb_, ALU.max, ALU.add, [px + "sc_ta", px + "sc_tb"], [px + "sc_ta"])
            act(tb_[:, 0, :], alB[0:rows, :], AF.Exp, ["alB", px + "sc_tb"], [px + "sc_tb"])
            stt(g_, ta, -1.0, bc(tb_[:, 0:1, :], [rows, nn, 4]), ALU.mult, ALU.mult, [px + "sc_ta", px + "sc_tb"], [px + "sc_g"])

        scalars(128, ba, NT, sB)
        bi = nextbank()
        mm(bank(bi)[:, 0:64], U_f, sB["g"], True, True, ["U_f", "sc_g"], bkeys(bi))
        cp("dve", sB["gc"], bank(bi)[:, 0:64], bkeys(bi), ["sc_gc"])
        ts("dve", sB["ngc"], sB["gc"], -1.0, None, ALU.mult, None, ["sc_gc"], ["sc_ngc"])
        bi = nextbank()
        mm(bank(bi)[:, 0:64], ones_f, sB["g"], True, True, ["ones_f", "sc_g"], bkeys(bi))
        cp("dve", sB["gl"], bank(bi)[:, 0:64], bkeys(bi), ["sc_gl"])
        act(sB["egl"], sB["gl"], AF.Exp, ["sc_gl"], ["sc_egl"])
        tt("dve", sB["ekd"], sB["gl"], sB["gc"], ALU.subtract, ["sc_gl", "sc_gc"], ["sc_ekd"])
        act(sB["ekd"], sB["ekd"], AF.Exp, ["sc_ekd"], ["sc_ekd"])
        act(sB["bge"], sB["gc"], AF.Exp, ["sc_gc"], ["sc_bge"])
        tt("dve", sB["bge"], sB["bge"], sB["beta"], ALU.mult, ["sc_bge", "sc_beta"], ["sc_bge"])
        ts("dve", sB["nbeta"], sB["beta"], -1.0, None, ALU.mult, None, ["sc_beta"], ["sc_nbeta"])
        scalars(TS, bas.rearrange("p (n c) -> p n c", n=1), 1, sS, "s")

        for h in range(4):
            if h == 0:
                assert slab_end + 2 * 2048 + 2 * 2048 + 3 * 1024 + 2 * 1024 <= AW - 7000
                S.barrier()
            kv_transposes(h)
            if h == 3:
                for h_ in range(4):
                    for s4 in range(4):
                        S.dma("sp", S_all[:, 16 * h_ + 4 * s4:16 * h_ + 4 * s4 + 4, :],
                              sd_d[4 * s4:4 * s4 + 4, h_].rearrange("s k v -> k s v"), writes=SLABK + ["S_all%d" % h_])
            precompute(h)
            if h < 3:
                drain(recurrence_gen(h), slab_gen(h + 1, [0, 2], 0), slab_gen(h + 1, [1, 3], 1), weights=[1, 2, 2])
            else:
                drain(recurrence_gen(h))
            outnorm(h)

        S.dma("sp", scp_d, xn19[16:19, :], reads=["xn19_%d_%d" % (p_, h_) for p_ in range(3) for h_ in range(4)])

        if STOP <= 1:
            S.emit()
            return nc
        S.barrier()
        A.off = slab_end
        pm = A.f32(516)
        markA2 = A.off
        cwB = A.f32(4, 1536)
        scs_sb = A.f32(4, 1536)
        prod = cwB
        qkv = A.f32(1536)
        pk = A.f32(4, 514)
        r8 = A.f32(8)
        XN = ["xn19_%d_%d" % (p_, h_) for p_ in range(3) for h_ in range(4)]
        S.dma("act", cwB[0:TS].rearrange("p j c -> p (j c)"), conv_w_d.rearrange("j c -> (j c)").partition_broadcast(TS), writes=["cwB"])
        S.dma("sp", scs_sb[0:TS, 0:3, :], sc_d, writes=["scs03"])
        cp("dve", scs_sb[0:TS, 3, :], xn19[0:TS, :], XN, ["scs3"])
        S.dma("sp", scs_d, scs_sb[0:TS, 1:4, :], reads=["scs03", "scs3"])
        tt("dve", prod[0:TS], scs_sb[0:TS], cwB[0:TS], ALU.mult, ["scs03", "scs3", "cwB"], ["prod", "cwB"])
        tt("dve", qkv[0:TS], prod[0:TS, 0, :], prod[0:TS, 1, :], ALU.add, ["prod"], ["qkv"])
        tt("dve", qkv[0:TS], qkv[0:TS], prod[0:TS, 2, :], ALU.add, ["prod", "qkv"], ["qkv"])
        tt("dve", qkv[0:TS], qkv[0:TS], prod[0:TS, 3, :], ALU.add, ["prod", "qkv"], ["qkv"])
        act(qkv[0:TS], qkv[0:TS], AF.Silu, ["qkv"], ["qkv"])
        sqv = prod[0:TS, 0, 0:1024]
        tt("dve", sqv, qkv[0:TS, 0:1024], qkv[0:TS, 0:1024], ALU.mult, ["qkv", "prod"], ["prod"])
        S.op("dve", lambda e: e.tensor_reduce(out=r8[0:TS, :], in_=sqv.rearrange("p (a b) -> p a b", b=128), axis=AX.X,
                                              op=ALU.add), ["prod"], ["r8"])
        ts("dve", r8[0:TS, :], r8[0:TS, :], 1.0, EPS, ALU.mult, ALU.add, ["r8"], ["r8"])
        tt("pool", r8[0:TS, :], r8[0:TS, :], bc(neghalf[0:TS, 0:1], [TS, 8]), ALU.pow, ["r8", "neghalf"], ["r8"])
        ts("dve", r8[0:TS, 0:4], r8[0:TS, 0:4], 128.0 ** -0.5, None, ALU.mult, None, ["r8"], ["r8"])
        q3 = qkv[0:TS, :].rearrange("p (a h d) -> p a h d", a=3, h=4)
        tt("dve", pk[0:TS, :, 0:128], q3[:, 1], bc(r8[0:TS, 4:8].unsqueeze(2), [TS, 4, 128]), ALU.mult, ["qkv", "r8"], ["pk"])
        tt("dve", pk[0:TS, :, 128:256], q3[:, 0], bc(r8[0:TS, 0:4].unsqueeze(2), [TS, 4, 128]), ALU.mult, ["qkv", "r8", "pk"], ["pk"])
        cp("dve", pk[0:TS, :, 256:384], q3[:, 2], ["qkv", "pk"], ["pk"])
        act(pk[0:TS, :, 384:512], ags[0:TS, :].rearrange("p (h d) -> p h d", h=4), AF.Silu,
            ["ags%d" % h_ for h_ in range(4)] + ["pk"], ["pk"])
        cp("dve", pk[0:TS, :, 512:513], sS["beta"][0:TS, :].unsqueeze(2), ["ssc_beta", "pk"], ["pk"])
        act(pk[0:TS, :, 513:514], sS["g"][0:TS, :].unsqueeze(2), AF.Exp, ["ssc_g", "pk"], ["pk"])
        for h in range(4):
            S.dma("sp" if h % 2 == 0 else "act", pm[16 * h:16 * h + 16, 0:514], pk[0:TS, h, :], reads=["pk"], writes=["pm%d" % h])
        PMK = ["pm%d" % h for h in range(4)]
        if STOP <= 2:
            S.emit()
            return nc
        S.barrier()
        A.off = markA2
        KQm = A.f32(64, 128)
        Kmask = A.f32(64, 128)
        SkSq = A.f32(128)
        Sq_pm = A.f32(128)
        wk1 = A.f32(128)
        wk2 = A.f32(128)
        vnew_pm = A.f32(128)
        o_pm = A.f32(128)
        angB = A.f32(128)
        sml = A.f32(8)
        aB = A.f32(64)
        dga = A.f32(64)
        jk = A.f32(128)
        SA = ["S_all%d" % h for h in range(4)]
        S.dma("sp", angB[0:64, :], a_norm_g_d.partition_broadcast(64), writes=["angB"])
        memset("pool", KQm, 0.0, ["KQm"])
        bi = nextbank()
        tr(bank(bi)[:, 0:64], pm[0:64, 0:128], ident_f[0:64, 0:64], PMK + ["ident_f"], bkeys(bi, 0, 1))
        tr(bank(bi)[:, 128:192], pm[0:64, 128:256], ident_f[0:64, 0:64], PMK + ["ident_f"], bkeys(bi, 1, 2))
        cp("dve", custom(KQm, 0, [[129, 64]]), bank(bi)[:, 0:64], bkeys(bi, 0, 1) + ["KQm"], ["KQm"])
        cp("dve", custom(KQm, 64, [[129, 64]]), bank(bi)[:, 128:192], bkeys(bi, 1, 2) + ["KQm"], ["KQm"])
        bi = nextbank()
        for p in range(64):
            mm(bank(bi)[:, 0:128], KQm[:, p, :], S_all[:, p, :], p == 0, p == 63, ["KQm"] + SA, bkeys(bi, 0, 1))
        a_c = pm[0:64, 513:514]
        memset("pool", Kmask, 0.0, ["Kmask"])
        memset("pool", dga, 0.0, ["dga"])
        cp("dve", Kmask[0:64], bc(pm[0:64, 0:128].unsqueeze(1), [64, 64, 128]), PMK + ["Kmask"], ["Kmask"])
        tt("dve", Kmask[0:64], Kmask[0:64], bc(ident_f[0:64, 0:64].unsqueeze(2), [64, 64, 128]), ALU.mult,
           ["Kmask", "ident_f"], ["Kmask"])
        ts("dve", dga[0:64, :], ident_f[0:64, 0:64], a_c, None, ALU.mult, None, ["ident_f", "dga"] + PMK, ["dga"])
        bi_a = nextbank()
        mm(bank(bi_a)[:, 0:64], ones_f, dga, True, True, ["ones_f", "dga"], bkeys(bi_a, 0, 1))
        cp("dve", aB, bank(bi_a)[:, 0:64], bkeys(bi_a, 0, 1), ["aB"])
        cp("dve", SkSq, bank(bi)[:, 0:128], bkeys(bi, 0, 1), ["SkSq"])
        S.dma("sp", Sq_pm[0:64, :], SkSq[64:128, :], reads=["SkSq"], writes=["Sq_pm"])
        if STOP <= 2.3:
            S.emit()
            return nc
        a_c = pm[0:64, 513:514]
        beta_c = pm[0:64, 512:513]
        ts("dve", sml[0:64, 0:1], beta_c, -1.0, None, ALU.mult, None, PMK, ["sml0"])
        stt(wk1[0:64, :], SkSq[0:64, :], a_c, pm[0:64, 256:384], ALU.mult, ALU.subtract, ["SkSq"] + PMK, ["wk1"])
        memset("pool", vnew_pm, 0.0, ["vnew_pm"])
        ts("dve", vnew_pm[0:64, :], wk1[0:64, :], sml[0:64, 0:1], None, ALU.mult, None, ["wk1", "sml0", "vnew_pm"], ["vnew_pm"])
        stt(jk[0:64, :], pm[0:64, 0:128], 1.0, pm[0:64, 128:256], ALU.mult, ALU.mult, PMK, ["jk", "sml1"],
            accum_out=sml[0:64, 1:2])
        ts("dve", wk2[0:64, :], Sq_pm[0:64, :], a_c, None, ALU.mult, None, ["Sq_pm"] + PMK, ["wk2"])
        stt(o_pm[0:64, :], vnew_pm[0:64, :], sml[0:64, 1:2], wk2[0:64, :], ALU.mult, ALU.add, ["vnew_pm", "sml1", "wk2"], ["o_pm"])
        stt(jk[0:64, :], o_pm[0:64, :], 1.0, o_pm[0:64, :], ALU.mult, ALU.mult, ["o_pm", "jk"], ["jk", "sml2"],
            accum_out=sml[0:64, 2:3])
        rsqrt_col(sml[0:64, 3:4], sml[0:64, 2:3], 64, 1.0 / 128, ["sml2"], ["sml3"])
        stt(wk1[0:64, :], o_pm[0:64, :], sml[0:64, 3:4], angB[0:64, :], ALU.mult, ALU.mult, ["o_pm", "sml3", "angB", "wk1"], ["wk1"])
        tt("dve", wk1[0:64, :], wk1[0:64, :], pm[0:64, 384:512], ALU.mult, ["wk1"] + PMK, ["wk1"])
        bi = nextbank()
        tr(bank(bi)[:, 0:64], wk1[0:64, :], ident_f[0:64, 0:64], ["wk1", "ident_f"], bkeys(bi, 0, 1))
        cp("dve", yaT[:, :, 2048:2064], bank(bi)[:, 0:64].rearrange("p (h s) -> p h s", h=4), bkeys(bi, 0, 1), ["yaT_s"])
        if STOP <= 2.6:
            S.emit()
            return nc
        if STOP <= 2.8:
            S.emit()
            return nc
        for p in range(64):
            bi = nextbank()
            qq = 0
            mm(bank(bi)[:, qq * 128:(qq + 1) * 128], Kmask[:, p, :], vnew_pm, True, True, ["Kmask", "vnew_pm"],
               bkeys(bi, qq, qq + 1))
            stt(S_all[:, p, :], S_all[:, p, :], aB[:, p:p + 1], bank(bi)[:, qq * 128:(qq + 1) * 128], ALU.mult, ALU.add,
                SA + ["aB"] + bkeys(bi, qq, qq + 1), ["Snew%d" % p])
        if STOP <= 2.9:
            S.emit()
            return nc
        for h in range(4):
            for s4 in range(4):
                S.dma("sp" if (h + s4) % 2 == 0 else "act", sds_d[4 * s4:4 * s4 + 4, h].rearrange("s k v -> k s v"),
                      S_all[:, 16 * h + 4 * s4:16 * h + 4 * s4 + 4, :],
                      reads=["Snew%d" % p for p in range(16 * h + 4 * s4, 16 * h + 4 * s4 + 4)])

        if STOP <= 3:
            S.emit()
            return nc
        S.barrier()
        A.off = mark0
        A.top = TOP_YC
        cgT = A.bf16(4, 2064)
        qsT = A.f32(64)
        Ks = [A.f32(2, 512) for _ in range(4)]

        def k_load(s_):
            S.dma("sp", Ks[s_ % 4], cmk_d[s_].rearrange("(mt m) c -> m mt c", m=128), writes=["Ks%d" % (s_ % 4)])

        for s_ in range(4):
            k_load(s_)
        markC1 = A.off
        wC = [A.bf16(8, 512) for _ in range(2)]
        xt = [A.f32(1024), A.f32(1024)]
        xs_bf = [A.bf16(1024), A.bf16(1024)]
        junk = A.bf16(1024)
        memnT = A.bf16(8, 256)
        mkT_b = A.bf16(4, 256)
        Vm_b = A.bf16(2, 512)
        mo_f = [A.f32(512), A.f32(512)]
        cqT = A.bf16(4, 2064)
        Pb = [A.bf16(512) for _ in range(4)]
        rden = [A.f32(512), A.f32(512)]
        ot = [A.f32(512), A.f32(512)]
        for mt in range(2):
            norm_T(mem_d[mt * 128:(mt + 1) * 128, :], 128, mgT, memnT, mt * 128, "memnT", 17 + mt, xt, xs_bf, junk)
        load_w(wC[0], w_mkv_d[:, 0:512], "wC0")
        load_w(wC[1], w_mkv_d[:, 512:1024], "wC1")
        for kv in range(2):
            for mt in range(2):
                bi = nextbank()
                proj_tm(wC[kv], 0, 512, mt * 128, (mt + 1) * 128, bank(bi), ["wC%d" % kv, "memnT"], bkeys(bi), src=memnT)
                mo = mo_f[mt]
                cp("act", mo, bank(bi), bkeys(bi), ["mo%d" % mt])
                S.dma("sp", (mkp_d if kv == 0 else mvp_d)[mt * 128:(mt + 1) * 128, :], mo, reads=["mo%d" % mt])
                if kv == 1:
                    cp("dve", Vm_b[:, mt, :], bank(bi), bkeys(bi), ["Vm_b"])
        for h in range(4):
            bi = nextbank()
            proj_fm(wC[0], h * 128, (h + 1) * 128, 0, 256, bank(bi)[:, 0:256], ["wC0", "memnT"], bkeys(bi, 0, 2), src=memnT)
            cp("dve", mkT_b[:, h, :], bank(bi)[:, 0:256], bkeys(bi, 0, 2), ["mkT_b"])
        load_w(wC[0], w_in_d[:, OFF["cq"]:OFF["cq"] + 512], "wC0")
        load_w(wC[1], w_in_d[:, OFF["cg"]:OFF["cg"] + 512], "wC1")
        TBL = [(tb * 512, (tb + 1) * 512, hkeys(tb)) for tb in range(4)] + [(2048, 2064, HS)]
        for h in range(4):
            for (t0, t1, hk) in TBL:
                w_ = t1 - t0
                bi = nextbank()
                proj_fm(wC[0], h * 128, (h + 1) * 128, t0, t1, bank(bi)[:, 0:w_], ["wC0"] + hk, bkeys(bi))
                act(cqT[:, h, t0:t1], bank(bi)[:, 0:w_], AF.Copy, bkeys(bi), ["cqT"], scale=128.0 ** -0.5)
                if t0 == 2048:
                    act(qsT[:, h * 16:(h + 1) * 16], bank(bi)[:, 0:w_], AF.Copy, bkeys(bi), ["qsT"], scale=128.0 ** -0.5)
                bi = nextbank()
                proj_fm(wC[1], h * 128, (h + 1) * 128, t0, t1, bank(bi)[:, 0:w_], ["wC1"] + hk, bkeys(bi))
                act(cgT[:, h, t0:t1], bank(bi)[:, 0:w_], AF.Silu, bkeys(bi), ["cgT"])
        for h in range(4):
            for tb in range(4):
                tsl = slice(tb * 512, (tb + 1) * 512)
                pbs = []
                for mt in range(2):
                    bi = nextbank()
                    mm(bank(bi), mkT_b[:, h, mt * 128:(mt + 1) * 128], cqT[:, h, tsl], True, True, ["mkT_b", "cqT"], bkeys(bi))
                    pi = (tb % 2) * 2 + mt
                    act(Pb[pi], bank(bi), AF.Exp, bkeys(bi), ["Pb%d" % pi])
                    pbs.append(pi)
                bo = nextbank()
                bd = nextbank()
                for mt in range(2):
                    mm(bank(bo), Vm_b[:, mt, h * 128:(h + 1) * 128], Pb[pbs[mt]], mt == 0, mt == 1, ["Vm_b", "Pb%d" % pbs[mt]], bkeys(bo))
                for mt in range(2):
                    mm(bank(bd), ones_b, Pb[pbs[mt]], mt == 0, mt == 1, ["ones_b", "Pb%d" % pbs[mt]], bkeys(bd))
                rd = rden[tb % 2]
                rk = "rden%d" % (tb % 2)
                act(rd, bank(bd), AF.Ln, bkeys(bd), [rk])
                act(rd, rd, AF.Exp, [rk], [rk], scale=-1.0)
                o_ = ot[tb % 2]
                ok = "ot%d" % (tb % 2)
                tt("dve", o_, bank(bo), rd, ALU.mult, bkeys(bo) + [rk], [ok])
                tt("pool", ycT[:, h, tsl], o_, cgT[:, h, tsl], ALU.mult, [ok, "cgT"], ["ycT"])
        if STOP <= 4:
            S.emit()
            return nc
        S.barrier()
        A.off = markC1
        Qm = A.f32(64, 64)
        Vs = [A.f32(2, 512) for _ in range(4)]

        def v_load(s_):
            S.dma("act", Vs[s_ % 4], cmv_d[s_].rearrange("(mt m) c -> m mt c", m=128), writes=["Vs%d" % (s_ % 4)])

        for s_ in range(4):
            v_load(s_)
        KTs = [A.f32(4, 256), A.f32(4, 256)]
        Pf = A.f32(256)
        PTm = A.f32(2, 16 * 64)
        selm = A.f32(16 * 64)
        hm = A.f32(4)
        smx = A.f32(8)
        opm = A.f32(128)
        memset("pool", Qm, 0.0, ["Qm"])
        cp("dve", custom(Qm, 0, [[65, 64]]), qsT, ["qsT", "Qm"], ["Qm"])
        memset("pool", selm, 1.0, ["selm"])
        asel(selm, selm, [[1, 16], [0, 4], [-1, 16]], ALU.is_equal, 0.0, 0, 0, ["selm"], ["selm"])
        memset("pool", hm[0:64, :], 1.0, ["hm"])
        asel(hm[0:64, :], hm[0:64, :], [[-16, 4]], ALU.is_ge, 0.0, 0, 1, ["hm"], ["hm"])
        asel(hm[0:64, :], hm[0:64, :], [[16, 4]], ALU.is_ge, 0.0, 15, -1, ["hm"], ["hm"])
        bsc = 7
        cnt = 0
        def katt_T(s):
            kb = Ks[s % 4]
            kk = "Ks%d" % (s % 4)
            if s >= 1 and s + 3 < TS:
                k_load(s + 3)
            b1, b2 = (5, 6) if s % 2 == 0 else (3, 4)
            for h in range(4):
                for mt in range(2):
                    bb = b1 if h < 2 else b2
                    off = (h % 2) * 256 + mt * 128
                    tr(bank(bb)[:, off:off + 128], kb[:, mt, h * 128:(h + 1) * 128], ident_f, [kk, "ident_f"], bkeys(bb))
            kt = KTs[s % 2]
            ktk = "KTs%d" % (s % 2)
            cp("act", kt[:, 0:2, :].rearrange("p a b -> p (a b)"), bank(b1), bkeys(b1), [ktk + "a"])
            cp("dve", kt[:, 2:4, :].rearrange("p a b -> p (a b)"), bank(b2), bkeys(b2), [ktk + "b"])

        def katt_M(s):
            kt = KTs[s % 2]
            ktk = "KTs%d" % (s % 2)
            for h in range(4):
                p = h * 16 + s
                c_ = s * 4 + h
                mm(bank(bsc)[0:64, 0:256], Qm[:, p, :], kt[:, h, :], c_ == 0, c_ == 63, ["Qm", ktk + ("a" if h < 2 else "b")], bkeys(bsc))

        katt_T(0)
        for s in range(TS):
            if s + 1 < TS:
                katt_T(s + 1)
            katt_M(s)
        S.op("dve", lambda e: e.tensor_reduce(out=smx[0:64, 0:1], in_=bank(bsc)[0:64, 0:256], axis=AX.X, op=ALU.max),
             bkeys(bsc), ["smx0"])
        ts("dve", smx[0:64, 1:2], smx[0:64, 0:1], -1.0, None, ALU.mult, None, ["smx0"], ["smx1"])
        act(Pf[0:64, :], bank(bsc)[0:64, 0:256], AF.Exp, bkeys(bsc) + ["smx1"], ["Pf", "smx2"], bias=smx[0:64, 1:2],
            accum_out=smx[0:64, 2:3])
        S.op("dve", lambda e: e.reciprocal(out=smx[0:64, 3:4], in_=smx[0:64, 2:3]), ["smx2"], ["smx3"])
        bi = nextbank()
        for mt in range(2):
            tr(bank(bi)[:, mt * 64:(mt + 1) * 64], Pf[0:64, mt * 128:(mt + 1) * 128], ident_f[0:64, 0:64], ["Pf", "ident_f"],
               bkeys(bi, 0, 1))
        for mt in range(2):
            tt("dve", PTm[:, mt, :].rearrange("p (s q) -> p s q", s=16),
               bc(bank(bi)[:, mt * 64:(mt + 1) * 64].unsqueeze(1), [128, 16, 64]),
               selm.rearrange("p (s q) -> p s q", s=16), ALU.mult, bkeys(bi, 0, 1) + ["selm"], ["PTm"])
        cnt = 0
        for s in range(TS):
            vb = Vs[s % 4]
            vk = "Vs%d" % (s % 4)
            if s >= 1 and s + 3 < TS:
                v_load(s + 3)
            for mt in range(2):
                mm(bank(bsc)[0:64, :], PTm[:, mt, s * 64:(s + 1) * 64], vb[:, mt, :], cnt == 0, cnt == 31, ["PTm", vk], bkeys(bsc))
                cnt += 1
        ts("dve", opm[0:64, :], bank(bsc)[0:64, 0:128], hm[0:64, 0:1], None, ALU.mult, None, bkeys(bsc) + ["hm"], ["opm"])
        for h2 in range(1, 4):
            stt(opm[0:64, :], bank(bsc)[0:64, h2 * 128:(h2 + 1) * 128], hm[0:64, h2:h2 + 1], opm[0:64, :], ALU.mult, ALU.add,
                bkeys(bsc) + ["hm", "opm"], ["opm"])
        ts("dve", opm[0:64, :], opm[0:64, :], smx[0:64, 3:4], None, ALU.mult, None, ["opm", "smx3"], ["opm"])
        bi = nextbank()
        tr(bank(bi)[:, 0:64], opm[0:64, :], ident_f[0:64, 0:64], ["opm", "ident_f"], bkeys(bi, 0, 1))
        tt("dve", ycT[:, :, 2048:2064], bank(bi)[:, 0:64].rearrange("p (h s) -> p h s", h=4), cgT[:, :, 2048:2064], ALU.mult,
           bkeys(bi, 0, 1) + ["cgT"], ["ycT_s"])

        if STOP <= 5:
            S.emit()
            return nc
        S.barrier()
        A.off = mark0
        A.top = TOP_YB
        wB = [A.bf16(8, 512) for _ in range(3)]
        vn_bf = A.bf16(16, 512)
        ug = A.bf16(4, 2048)
        lngB = A.f32(512)
        lnbB = A.f32(512)
        ws_f = A.f32(4, 128)
        ws_b = A.bf16(4, 128)
        wsT_b = A.bf16(4, 128)
        bsp = A.f32(512)
        tln = [A.f32(512), A.f32(512)]
        sgt = [A.bf16(512), A.bf16(512)]
        bst = A.f32(20, 8)
        vn_s = A.f32(512)
        bus = A.f32(512)
        bgs = A.f32(512)
        ws00 = A.f32(4)
        b0 = A.f32(4)
        load_w(wB[0], w_in_d[:, OFF["bv"]:OFF["bv"] + 512], "wB0")
        load_w(wB[1], w_in_d[:, OFF["bg"]:OFF["bg"] + 512], "wB1")
        load_w(wB[2], w_in_d[:, OFF["bu"]:OFF["bu"] + 512], "wB2")
        S.dma("sp", lngB, ln_v_g_d.partition_broadcast(128), writes=["lngB"])
        S.dma("sp", lnbB, ln_v_b_d.partition_broadcast(128), writes=["lnbB"])
        S.dma("act", ws_f, w_sp_d.rearrange("g t s -> t g s"), writes=["ws_f"])
        S.dma("sp", bsp[0:1, :], b_sp_d.rearrange("g t -> (g t)").partition_broadcast(1), writes=["bsp"])
        S.dma("sp", ws00[0:TS, :], bass.AP(tensor=w_sp_d.tensor, offset=w_sp_d.offset, ap=[[0, TS], [128 * 128, 4]]),
              writes=["ws00"], allow_slow_non_contiguous=True)
        S.dma("sp", b0[0:TS, :], bass.AP(tensor=b_sp_d.tensor, offset=b_sp_d.offset, ap=[[0, TS], [128, 4]]),
              writes=["b0"], allow_slow_non_contiguous=True)
        for g_ in range(4):
            asel(ws_f[:, g_, :], ws_f[:, g_, :], [[-1, 128]], ALU.is_ge, 0.0, 0, 1, ["ws_f"], ["ws_f"])
        cp("dve", ws_b, ws_f, ["ws_f"], ["ws_b"])
        bi = nextbank()
        pT = bank(bi).bitcast(BF16)
        for g_ in range(4):
            tr(pT[:, g_ * 128:(g_ + 1) * 128], ws_b[:, g_, :], ident_b, ["ws_b", "ident_b"], bkeys(bi))
        cp("dve", wsT_b.rearrange("p g t -> p (g t)"), pT[:, 0:512], bkeys(bi), ["wsT_b"])

        def ln_A(rows, src_ps, rk, idx):
            S.op("dve", lambda e: e.bn_stats(out=bst[0:rows, idx, 0:6], in_=src_ps), rk, ["bst%d" % idx])
            S.op("dve", lambda e: e.bn_aggr(out=bst[0:rows, idx, 6:8], in_=bst[0:rows, idx, 0:6]), ["bst%d" % idx], ["bag%d" % idx])
            rsqrt_col(bst[0:rows, idx, 0:1], bst[0:rows, idx, 7:8], rows, 1.0, ["bag%d" % idx], ["brs%d" % idx])

        def ln_B(rows, src_ps, rk, idx, out_ap, okey):
            t_ = tln[idx % 2][0:rows, :]
            tk = "tln%d" % (idx % 2)
            ts("dve", t_, src_ps, bst[0:rows, idx, 6:7], bst[0:rows, idx, 0:1], ALU.subtract, ALU.mult,
               rk + ["bag%d" % idx, "brs%d" % idx], [tk])
            tt("pool", t_, t_, lngB[0:rows, :], ALU.mult, [tk, "lngB"], [tk])
            tt("pool", out_ap, t_, lnbB[0:rows, :], ALU.add, [tk, "lnbB"], [okey])

        prevb = None
        for n in range(NT + 1):
            bi = nextbank()
            if n < NT:
                proj_tm(wB[0], 0, 512, n * 128, (n + 1) * 128, bank(bi), ["wB0", "hT%d" % n], bkeys(bi))
                cur = (128, bank(bi), bkeys(bi), n, vn_bf[:, n, :], "vn%d" % n)
            else:
                proj_tm(wB[0], 0, 512, 2048, 2064, bank(bi)[0:TS, :], ["wB0"] + HS, bkeys(bi))
                cur = (TS, bank(bi)[0:TS, :], bkeys(bi), 16, vn_s[0:TS, :], "vn_s")
            ln_A(*cur[:4])
            if prevb is not None:
                ln_B(*prevb)
            prevb = cur
        ln_B(*prevb)
        S.dma("sp", cvs_d, vn_s[0:TS, :], reads=["vn_s"])
        for c in range(4):
            for tb in range(4):
                tsl = slice(tb * 512, (tb + 1) * 512)
                b1 = nextbank()
                proj_fm(wB[1], c * 128, (c + 1) * 128, tb * 512, (tb + 1) * 512, bank(b1), ["wB1"] + hkeys(tb), bkeys(b1))
                sg_ = sgt[(c * 4 + tb) % 2]
                sk = "sgt%d" % ((c * 4 + tb) % 2)
                act(sg_, bank(b1), AF.Silu, bkeys(b1), [sk])
                b2 = nextbank()
                proj_fm(wB[2], c * 128, (c + 1) * 128, tb * 512, (tb + 1) * 512, bank(b2), ["wB2"] + hkeys(tb), bkeys(b2))
                tt("dve", ug[:, c, tsl], bank(b2), sg_, ALU.mult, bkeys(b2) + [sk], ["ug"])
        b1 = nextbank()
        proj_tm(wB[1], 0, 512, 2048, 2064, bank(b1)[0:TS, :], ["wB1"] + HS, bkeys(b1))
        act(bgs[0:TS, :], bank(b1)[0:TS, :], AF.Silu, bkeys(b1), ["bgs"])
        b2 = nextbank()
        proj_tm(wB[2], 0, 512, 2048, 2064, bank(b2)[0:TS, :], ["wB2"] + HS, bkeys(b2))
        tt("dve", bus[0:TS, :], bank(b2)[0:TS, :], bgs[0:TS, :], ALU.mult, bkeys(b2) + ["bgs"], ["bus"])
        for n in range(NT):
            bi = nextbank()
            for g_ in range(4):
                o_ = bank(bi)[:, g_ * 128:(g_ + 1) * 128]
                mm(o_, vn_bf[:, n, g_ * 128:(g_ + 1) * 128], wsT_b[:, g_, :], True, False, ["vn%d" % n, "wsT_b"], bkeys(bi))
                mm(o_, ones_f[0:1, :], bsp[0:1, g_ * 128:(g_ + 1) * 128], False, True, ["ones_f", "bsp"], bkeys(bi))
            tt("dve", ybT[:, :, n * 128:(n + 1) * 128], bank(bi).rearrange("p (g t) -> p g t", g=4),
               ug[:, :, n * 128:(n + 1) * 128], ALU.mult, bkeys(bi) + ["ug"], ["ybT"])
        v4 = lambda ap: ap[0:TS, :].rearrange("p (g c) -> p g c", g=4)
        tt("dve", v4(vn_s), v4(vn_s), bc(ws00[0:TS, :].unsqueeze(2), [TS, 4, 128]), ALU.mult, ["vn_s", "ws00"], ["vn_s"])
        tt("dve", v4(vn_s), v4(vn_s), bc(b0[0:TS, :].unsqueeze(2), [TS, 4, 128]), ALU.add, ["vn_s", "b0"], ["vn_s"])
        tt("dve", bus[0:TS, :], bus[0:TS, :], vn_s[0:TS, :], ALU.mult, ["bus", "vn_s"], ["bus"])
        bi = nextbank()
        for g_ in range(4):
            tr(bank(bi)[:, g_ * 16:(g_ + 1) * 16], bus[0:TS, g_ * 128:(g_ + 1) * 128], ident_f[0:TS, 0:TS], ["bus", "ident_f"],
               bkeys(bi, 0, 1))
        cp("dve", ybT[:, :, 2048:2064], bank(bi)[:, 0:64].rearrange("p (g s) -> p g s", g=4), bkeys(bi, 0, 1), ["ybT_s"])

        if STOP <= 6:
            S.emit()
            return nc
        S.barrier()
        A.off = mark0
        A.top = TOP_YB
        mT = A.bf16(8, 2064)
        wo = A.bf16(8, 1024)
        fgB = A.f32(1024)
        markM1 = A.off
        wg = [[A.bf16(8, 128) for _ in range(3)] for _ in range(2)]
        wbr = [[A.bf16(4, 128) for _ in range(3)] for _ in range(2)]
        gsb = [[A.f32(512) for _ in range(3)] for _ in range(2)]
        t0b = [A.f32(512) for _ in range(2)]
        t1b = [A.f32(512) for _ in range(2)]
        t2b = [A.f32(512) for _ in range(2)]
        yT = [yaT, ybT, ycT]
        ykeys = [["yaT", "yaT_s"], ["ybT", "ybT_s"], ["ycT", "ycT_s"]]
        wbr_d = [w_bra_d, w_brb_d, w_brc_d]
        it = 0
        def load_dc(dc_):
            w2 = dc_ % 2
            for i in range(3):
                load_w(wg[w2][i], w_in_d[:, OFF["mg"] + i * 1024 + dc_ * 128: OFF["mg"] + i * 1024 + (dc_ + 1) * 128], "wg%d_%d" % (w2, i))
                load_w(wbr[w2][i], wbr_d[i][:, dc_ * 128:(dc_ + 1) * 128], "wbr%d_%d" % (w2, i))

        load_dc(0)
        for dc in range(8):
            ws_ = dc % 2
            if dc + 1 < 8:
                load_dc(dc + 1)
            if dc == 1:
                load_w(wo[:, :, 0:512], w_out_d[:, 0:512], "wo0")
                load_w(wo[:, :, 512:1024], w_out_d[:, 512:1024], "wo1")
                S.dma("sp", fgB, fng_d.partition_broadcast(128), writes=["fgB"])
            for (t0, t1, hk) in TBL:
                w_ = t1 - t0
                pp = it % 2
                it += 1
                for i in range(3):
                    bi = nextbank()
                    proj_fm(wg[ws_][i], 0, 128, t0, t1, bank(bi)[:, 0:w_], ["wg%d_%d" % (ws_, i)] + hk, bkeys(bi))
                    act(gsb[pp][i][:, 0:w_], bank(bi)[:, 0:w_], AF.Sigmoid, bkeys(bi) + ["bgT"], ["gsb%d_%d" % (pp, i)],
                        bias=bgT[:, i, dc:dc + 1])
                tmp = [t0b[pp], t1b[pp], t2b[pp]]
                tk = ["t0b%d" % pp, "t1b%d" % pp, "t2b%d" % pp]
                for i in range(3):
                    bi = nextbank()
                    proj_fm(wbr[ws_][i], 0, 128, t0, t1, bank(bi)[:, 0:w_], ["wbr%d_%d" % (ws_, i)] + ykeys[i], bkeys(bi), K=4, src=yT[i])
                    tt("dve", tmp[i][:, 0:w_], bank(bi)[:, 0:w_], gsb[pp][i][:, 0:w_], ALU.mult, bkeys(bi) + ["gsb%d_%d" % (pp, i)], [tk[i]])
                tt("pool", tmp[0][:, 0:w_], tmp[0][:, 0:w_], tmp[1][:, 0:w_], ALU.add, [tk[0], tk[1]], [tk[0]])
                tt("pool", mT[:, dc, t0:t1], tmp[0][:, 0:w_], tmp[2][:, 0:w_], ALU.add, [tk[0], tk[2]], ["mT%d" % dc])
        MK = ["mT%d" % dc for dc in range(8)]
        if STOP <= 7:
            S.emit()
            return nc
        S.barrier()
        A.off = markM1
        A.top = AW
        xo = [A.f32(1024), A.f32(1024)]
        oo = [A.f32(1024), A.f32(1024)]
        junk2 = A.bf16(1024)
        ost = A.f32(20, 4)
        for n in range(NT + 1):
            rows = 128 if n < NT else TS
            t0 = n * 128
            b = n % 2
            src = x_d[n * 128:(n + 1) * 128, :] if n < NT else xs_d
            dst = y_d[n * 128:(n + 1) * 128, :] if n < NT else ys_d
            S.dma("act", xo[b][0:rows, :], src, writes=["xo%d" % b])
            for half in range(2):
                bi = nextbank()
                proj_tm(wo, half * 512, (half + 1) * 512, t0, t0 + rows, bank(bi)[0:rows, :], ["wo%d" % half] + MK, bkeys(bi), src=mT)
                tt("dve", oo[b][0:rows, half * 512:(half + 1) * 512], bank(bi)[0:rows, :], xo[b][0:rows, half * 512:(half + 1) * 512],
                   ALU.add, bkeys(bi) + ["xo%d" % b], ["oo%d_%d" % (b, half)])
            ok = ["oo%d_0" % b, "oo%d_1" % b]
            act(junk2[0:rows, :], oo[b][0:rows, :], AF.Square, ok, ["junk2", "ost%d" % n], accum_out=ost[0:rows, n, 0:1])
            rsqrt_col(ost[0:rows, n, 1:2], ost[0:rows, n, 0:1], rows, 1.0 / 1024, ["ost%d" % n], ["ostr%d" % n])
            stt(oo[b][0:rows, :], oo[b][0:rows, :], ost[0:rows, n, 1:2], fgB[0:rows, :], ALU.mult, ALU.mult,
                ok + ["ostr%d" % n, "fgB"], ["oof%d" % b] + ok)
            S.dma("sp", dst, oo[b][0:rows, :], reads=["oof%d" % b] + ok)
        S.emit()
    return nc


_CACHE = {}


def kernel(x_prompt, x_sample, cache_mem_k, cache_mem_v, state_delta, state_conv, mem_prompt,
           norm_g, w_in, conv_w, a_log, dt_bias, a_norm_g, ln_v_g, ln_v_b, w_spatial, b_spatial,
           mem_norm_g, w_mem_kv, w_br_a, w_br_b, w_br_c, b_gate, w_out, final_norm_g):
    f = lambda a: np.ascontiguousarray(np.asarray(a, dtype=np.float32))
    n = 8
    if "nc" not in _CACHE:
        _CACHE["nc"] = build()
    nc = _CACHE["nc"]
    shared = dict(norm_g=f(norm_g)[0], w_in=f(w_in)[0], conv_w=f(conv_w)[0], a_log=f(a_log)[0], dt_bias=f(dt_bias)[0],
                  a_norm_g=f(a_norm_g)[0], ln_v_g=f(ln_v_g)[0], ln_v_b=f(ln_v_b)[0], w_spatial=f(w_spatial)[0],
                  b_spatial=f(b_spatial)[0], mem_norm_g=f(mem_norm_g)[0], w_mem_kv=f(w_mem_kv)[0], w_br_a=f(w_br_a)[0],
                  w_br_b=f(w_br_b)[0], w_br_c=f(w_br_c)[0], b_gate=f(b_gate)[0], w_out=f(w_out)[0],
                  final_norm_g=f(final_norm_g))
    xp, xs = f(x_prompt), f(x_sample)
    cmk, cmv, sd, sc, mem = f(cache_mem_k), f(cache_mem_v), f(state_delta), f(state_conv), f(mem_prompt)
    in_maps = []
    for b in range(n):
        sl = slice(16 * b, 16 * b + 16)
        m = dict(shared)
        m.update(x=xp[b], xs=xs[sl, 0, :], cmk=cmk[0, sl].reshape(16, 256, 512), cmv=cmv[0, sl].reshape(16, 256, 512),
                 sd=sd[0, sl], sc=sc[0, sl], mem=mem[b])
        in_maps.append(m)
    res = run_bass_kernel_spmd(nc, in_maps, core_ids=list(range(n)))
    R = res.results
    y_prompt = np.stack([R[b]["y"] for b in range(n)], 0)
    y_sample = np.concatenate([R[b]["ys"] for b in range(n)], 0)[:, None, :]
    sdp = np.stack([R[b]["sdp"] for b in range(n)], 0)[None]
    scp = np.stack([R[b]["scp"] for b in range(n)], 0)[None]
    mkp = np.stack([R[b]["mkp"].reshape(256, 4, 128) for b in range(n)], 0)[None]
    mvp = np.stack([R[b]["mvp"].reshape(256, 4, 128) for b in range(n)], 0)[None]
    sds = np.concatenate([R[b]["sds"] for b in range(n)], 0)[None]
    scs = np.concatenate([R[b]["scs"] for b in range(n)], 0)[None]
    cvs = np.concatenate([R[b]["cvs"] for b in range(n)], 0)[None, :, None, :]
    return (y_prompt.astype(np.float32), y_sample.astype(np.float32), sdp.astype(np.float32), scp.astype(np.float32),
            mkp.astype(np.float32), mvp.astype(np.float32), sds.astype(np.float32), scs.astype(np.float32),
            cvs.astype(np.float32))
```

```python
import contextlib
import os
import sys
import numpy as np
import concourse.bass as bass
import concourse.mybir as mybir
from concourse.bass_utils import run_bass_kernel_spmd

F32 = mybir.dt.float32
BF16 = mybir.dt.bfloat16
AF = mybir.ActivationFunctionType
ALU = mybir.AluOpType
AX = mybir.AxisListType

ENGS = ("pe", "dve", "act", "pool", "sp")
N_DMA_SEMS = 12
EPS = 1e-6
T = 2048
NT = 16
TS = 16
OFF = dict(aq=0, ak=512, av=1024, ab=1536, ag=1544, bu=2056, bv=2568, bg=3080, cq=3592, cg=4104, mg=4616)
BIG = 30000.0


class Sched:
    def __init__(self, nc):
        self.nc = nc
        self.ops = []
        self.last_write = {}
        self.readers = {}
        self.clock = {e: {} for e in ENGS}
        self.cnt = {e: 0 for e in ENGS}
        self.last_compute = {}
        self.dma_i = {e: 0 for e in ENGS}
        self.dma_last = {}
        self.per_eng = {e: [] for e in ENGS}
        self.pending_bar = {e: [] for e in ENGS}
        self.ever = set()

    def _add(self, eng, fn, reads, writes, is_dma):
        op = dict(id=len(self.ops), eng=eng, fn=fn, dma=is_dma, waits=[])
        fr = sys._getframe(2)
        op["where"] = []
        while fr is not None and len(op["where"]) < 4:
            op["where"].append(fr.f_lineno)
            fr = fr.f_back
        deps = []
        for b in reads:
            w = self.last_write.get(b)
            if w is not None:
                deps.append((w, "raw"))
            elif isinstance(b, str) and b.startswith("hT") and b not in self.ever:
                raise RuntimeError("read of %s recorded before its producer" % b)
        for b in writes:
            w = self.last_write.get(b)
            if w is not None:
                deps.append((w, "waw"))
            for r in self.readers.get(b, ()):
                deps.append((r, "war"))
        if self.pending_bar[eng]:
            for d in self.pending_bar[eng]:
                deps.append((d, "bar"))
            self.pending_bar[eng] = []
        if is_dma:
            i = self.dma_i[eng]
            self.dma_i[eng] += 1
            slot = i % N_DMA_SEMS
            prev = self.dma_last.get((eng, slot))
            if prev is not None:
                deps.append((prev, "slot"))
            op["sem"] = ("dma", eng, slot)
            op["val"] = 16 * (i // N_DMA_SEMS + 1)
            op["inc"] = 16
            self.dma_last[(eng, slot)] = op
        else:
            self.cnt[eng] += 1
            op["sem"] = ("eng", eng)
            op["val"] = self.cnt[eng]
            op["inc"] = 1
            self.last_compute[eng] = op
        clk = self.clock[eng]
        need = {}
        used = []
        for d, kind in deps:
            if (not d["dma"]) and (not is_dma) and d["eng"] == eng:
                if kind != "raw" or eng == "pe":
                    continue
            k, v = d["sem"], d["val"]
            if clk.get(k, 0) >= v:
                continue
            used.append(d)
            if need.get(k, 0) < v:
                need[k] = v
        if not os.environ.get("MK_NOSNAP"):
            for d in used:
                for kk, vv in d["snap"].items():
                    if clk.get(kk, 0) < vv:
                        clk[kk] = vv
        for k, v in need.items():
            if clk.get(k, 0) < v:
                clk[k] = v
            op["waits"].append((k, v))
        snap = dict(clk)
        if not is_dma:
            snap[op["sem"]] = op["val"]
        op["snap"] = snap
        for b in reads:
            self.readers.setdefault(b, []).append(op)
        for b in writes:
            self.last_write[b] = op
            self.readers[b] = []
            self.ever.add(b)
        self.ops.append(op)
        self.per_eng[eng].append(op)
        return op

    def op(self, eng, fn, reads=(), writes=()):
        pr = [k for k in reads if isinstance(k, str) and k.startswith("pb")]
        if pr:
            reads = [k for k in reads if k not in pr]
            writes = list(writes) + [k for k in pr if k not in writes]
        return self._add(eng, fn, tuple(reads), tuple(writes), False)

    def dma(self, eng, out, in_, reads=(), writes=(), **kw):
        def fn(e, out=out, in_=in_, kw=kw):
            return e.dma_start(out=out, in_=in_, **kw)
        return self._add(eng, fn, tuple(reads), tuple(writes), True)

    def barrier(self):
        outstanding = list(self.last_compute.values()) + list(self.dma_last.values())
        for e in ENGS:
            self.pending_bar[e] = list(outstanding)
        self.last_write = {}
        self.readers = {}

    def emit(self):
        nc = self.nc
        if os.environ.get("MK_VERBOSE"):
            print("SCHED counts", self.cnt, "dma", self.dma_i, flush=True)
        with contextlib.ExitStack() as st:
            sems = {}
            for e in ENGS:
                sems[("eng", e)] = st.enter_context(nc.semaphore("s_" + e))
                for s in range(N_DMA_SEMS):
                    if self.dma_i[e] > s:
                        sems[("dma", e, s)] = st.enter_context(nc.semaphore("d_%s_%d" % (e, s)))
            block = st.enter_context(nc.Block())
            hooks = dict(pe=block.tensor, dve=block.vector, act=block.scalar, pool=block.gpsimd, sp=block.sync)

            def make(e):
                ops = self.per_eng[e]

                def body(eng):
                    for op in ops:
                        for k, v in op["waits"]:
                            eng.wait_ge(sems[k], v)
                        try:
                            inst = op["fn"](eng)
                        except Exception:
                            print("FAILED OP recorded at lines", op["where"], "engine", e)
                            raise
                        inst.then_inc(sems[op["sem"]], op["inc"])
                    for s in range(N_DMA_SEMS):
                        last = self.dma_last.get((e, s))
                        if last is not None:
                            eng.wait_ge(sems[last["sem"]], last["val"])
                return body

            for e in ENGS:
                if self.per_eng[e]:
                    hooks[e](make(e))


def _prod(s):
    r = 1
    for v in s:
        r *= v
    return r


class Arena:
    def __init__(self, ap, total):
        self.ap, self.total, self.off = ap, total, 0
        self.top = total

    def at_f32(self, off, *shape):
        n = _prod(shape)
        assert off + n <= self.total
        return self._shape(self.ap[:, off:off + n], shape)

    def bf16_top(self, *shape):
        n = _prod(shape)
        w = (n + 3) // 4 * 2
        self.top -= w
        assert self.off <= self.top
        a = self.ap[:, self.top:self.top + w].bitcast(BF16)[:, 0:n]
        return self._shape(a, shape)

    def _shape(self, a, shape):
        if len(shape) == 1:
            return a
        if len(shape) == 2:
            return a.rearrange("p (a b) -> p a b", a=shape[0])
        if len(shape) == 3:
            return a.rearrange("p (a b c) -> p a b c", a=shape[0], b=shape[1])
        raise ValueError

    def f32(self, *shape):
        n = _prod(shape)
        n2 = (n + 1) // 2 * 2
        a = self.ap[:, self.off:self.off + n]
        self.off += n2
        assert self.off <= self.top, ("arena overflow", self.off, self.top)
        return self._shape(a, shape)

    def bf16(self, *shape):
        n = _prod(shape)
        w = (n + 3) // 4 * 2
        a = self.ap[:, self.off:self.off + w].bitcast(BF16)[:, 0:n]
        self.off += w
        assert self.off <= self.top, ("arena overflow", self.off, self.top)
        return self._shape(a, shape)


def build():
    nc = bass.Bass("TRN2", target_bir_lowering=False)

    def din(name, shape):
        return nc.dram_tensor(name, list(shape), F32, kind="ExternalInput").ap()

    def dout(name, shape):
        return nc.dram_tensor(name, list(shape), F32, kind="ExternalOutput").ap()

    x_d = din("x", (T, 1024))
    xs_d = din("xs", (TS, 1024))
    cmk_d = din("cmk", (TS, 256, 512))
    cmv_d = din("cmv", (TS, 256, 512))
    sd_d = din("sd", (TS, 4, 128, 128))
    sc_d = din("sc", (TS, 3, 1536))
    mem_d = din("mem", (256, 1024))
    norm_g_d = din("norm_g", (1024,))
    w_in_d = din("w_in", (1024, 7688))
    conv_w_d = din("conv_w", (4, 1536))
    a_log_d = din("a_log", (4,))
    dt_bias_d = din("dt_bias", (4,))
    a_norm_g_d = din("a_norm_g", (128,))
    ln_v_g_d = din("ln_v_g", (512,))
    ln_v_b_d = din("ln_v_b", (512,))
    w_sp_d = din("w_spatial", (4, 128, 128))
    b_sp_d = din("b_spatial", (4, 128))
    mem_norm_g_d = din("mem_norm_g", (1024,))
    w_mkv_d = din("w_mem_kv", (1024, 1024))
    w_bra_d = din("w_br_a", (512, 1024))
    w_brb_d = din("w_br_b", (512, 1024))
    w_brc_d = din("w_br_c", (512, 1024))
    b_gate_d = din("b_gate", (3, 1024))
    w_out_d = din("w_out", (1024, 1024))
    fng_d = din("final_norm_g", (1024,))

    y_d = dout("y", (T, 1024))
    ys_d = dout("ys", (TS, 1024))
    sdp_d = dout("sdp", (4, 128, 128))
    scp_d = dout("scp", (3, 1536))
    mkp_d = dout("mkp", (256, 512))
    mvp_d = dout("mvp", (256, 512))
    sds_d = dout("sds", (TS, 4, 128, 128))
    scs_d = dout("scs", (TS, 3, 1536))
    cvs_d = dout("cvs", (TS, 512))

    S = Sched(nc)
    AW = 49000
    STOP = float(os.environ.get("MK_STOP", "99"))
    with contextlib.ExitStack() as st:
        arena_t = st.enter_context(nc.sbuf_tensor("arena", [128, AW], F32))
        ps_t = st.enter_context(nc.psum_tensor("ps", [128, 4096], F32))
        A = Arena(arena_t[:, :], AW)

        def bank(i):
            return ps_t[:, i * 512:(i + 1) * 512]

        def bkeys(i, q0=0, q1=4):
            return ["pb%d" % i]

        rr = {"b": 0, "pool": list(range(8))}

        def nextbank():
            rr["b"] = (rr["b"] + 1) % len(rr["pool"])
            return rr["pool"][rr["b"]]

        def mm(out, lhsT, rhs, start, stop, reads, writes):
            S.op("pe", lambda e: e.matmul(out, lhsT=lhsT, rhs=rhs, start=start, stop=stop), reads, writes)

        def tr(out, in_, ident, reads, writes):
            S.op("pe", lambda e: e.transpose(out=out, in_=in_, identity=ident), reads, writes)

        def act(out, in_, func, reads, writes, **kw):
            S.op("act", lambda e: e.activation(out=out, in_=in_, func=func, **kw), reads, writes)

        def tt(eng, out, in0, in1, op, reads, writes):
            S.op(eng, lambda e: e.tensor_tensor(out=out, in0=in0, in1=in1, op=op), reads, writes)

        def ts(eng, out, in0, s1, s2, op0, op1, reads, writes):
            if op1 is None:
                S.op(eng, lambda e: e.tensor_scalar(out=out, in0=in0, scalar1=s1, scalar2=None, op0=op0), reads, writes)
            else:
                S.op(eng, lambda e: e.tensor_scalar(out=out, in0=in0, scalar1=s1, scalar2=s2, op0=op0, op1=op1), reads, writes)

        def stt(out, in0, scalar, in1, op0, op1, reads, writes, accum_out=None):
            S.op("dve", lambda e: e.scalar_tensor_tensor(out=out, in0=in0, scalar=scalar, in1=in1, op0=op0, op1=op1,
                                                         accum_out=accum_out), reads, writes)

        def cp(eng, out, in_, reads, writes):
            if eng == "act":
                S.op("act", lambda e: e.copy(out=out, in_=in_), reads, writes)
            else:
                S.op(eng, lambda e: e.tensor_copy(out=out, in_=in_), reads, writes)

        def memset(eng, ap, val, writes):
            S.op(eng, lambda e: e.memset(ap, val), (), writes)

        def asel(out, in_, pattern, cmp_op, fill, base, cm, reads, writes):
            S.op("pool", lambda e: e.affine_select(out=out, in_=in_, pattern=pattern, compare_op=cmp_op, fill=fill,
                                                   base=base, channel_multiplier=cm), reads, writes)

        def bc(ap, shape):
            return ap.broadcast_to(list(shape))

        def custom(base_ap, extra_off, dims):
            return bass.AP(tensor=base_ap.tensor, offset=base_ap.offset + extra_off, ap=[list(base_ap.ap[0])] + dims)

        evq = {"i": 0}

        def evac_eng():
            evq["i"] += 1
            return "act" if evq["i"] % 2 else "dve"

        ident_f = A.f32(128)
        ident_b = A.bf16(128)
        ones_f = A.f32(128)
        ones_b = A.bf16(128)
        U_f = A.f32(128)
        NEG_up = A.f32(128)
        POS_lo = A.f32(128)
        neghalf = A.f32(2)
        BD32 = A.bf16(128)
        M32o = A.bf16(128)
        M64o = A.bf16(128)
        gT = A.f32(8)
        mgT = A.f32(8)
        cwT = A.f32(12, 4)
        bgT = A.f32(3, 8)
        ang = A.f32(2)
        alB = A.f32(4)
        dtB = A.f32(4)
        stats = A.f32(20, 4)
        hT = A.bf16(8, 2068)
        yaT = A.bf16(4, 2064)
        ycT = A.bf16_top(4, 2064)
        ybT = A.bf16_top(4, 2064)
        TOP_YC = AW - (4 * 2064 + 3) // 4 * 2
        TOP_YB = A.top
        A.top = AW
        xn19 = A.f32(1536)
        ags = A.f32(512)
        ba = A.f32(16, 8)
        bas = A.f32(8)
        sc64 = {}
        for nm in ("beta", "nbeta", "g", "gc", "ngc", "gl", "egl", "ekd", "bge", "tmpa", "tmpb"):
            sc64[nm] = A.f32(64)
        sS = {nm: A.f32(4) for nm in ("beta", "g", "tmpa", "tmpb")}
        PERSIST = A.off

        memset("pool", ident_f, 0.0, ["ident_f"])
        asel(ident_f, ident_f, [[-1, 128]], ALU.not_equal, 1.0, 0, 1, ["ident_f"], ["ident_f"])
        cp("dve", ident_b, ident_f, ["ident_f"], ["ident_b"])
        memset("pool", ones_f, 1.0, ["ones_f"])
        memset("pool", ones_b, 1.0, ["ones_b"])
        memset("pool", U_f, 1.0, ["U_f"])
        asel(U_f, U_f, [[1, 128]], ALU.is_ge, 0.0, 0, -1, ["U_f"], ["U_f"])
        memset("pool", NEG_up, 0.0, ["NEG_up"])
        asel(NEG_up, NEG_up, [[1, 128]], ALU.is_ge, -BIG, 0, -1, ["NEG_up"], ["NEG_up"])
        memset("pool", POS_lo, 0.0, ["POS_lo"])
        asel(POS_lo, POS_lo, [[-1, 128]], ALU.is_gt, BIG, 0, 1, ["POS_lo"], ["POS_lo"])
        memset("pool", neghalf, -0.5, ["neghalf"])
        memset("pool", BD32, 0.0, ["BD32"])
        memset("pool", M32o, 0.0, ["M32o"])
        memset("pool", M64o, 0.0, ["M64o"])
        for q_ in range(2):
            S.op("pool", lambda e, q_=q_: e.memset(M32o[64 * q_:64 * q_ + 64, 64 * q_:64 * q_ + 64], 1.0), ["M32o"], ["M32o"])
        for q_ in range(4):
            S.op("pool", lambda e, q_=q_: e.memset(BD32[32 * q_:32 * q_ + 32, 32 * q_:32 * q_ + 32], 1.0), ["BD32"], ["BD32"])
            S.op("pool", lambda e, q_=q_: e.memset(M32o[32 * q_:32 * q_ + 32, 32 * q_:32 * q_ + 32], 0.0), ["M32o"], ["M32o"])
        S.op("pool", lambda e: e.memset(M64o[64:128, 0:64], 1.0), ["M64o"], ["M64o"])
        memset("pool", stats, 0.0, ["stats"])
        cst_g = A.at_f32(40000, 128)
        cst_m = A.at_f32(40128, 128)
        cst_c = A.at_f32(40256, 4, 128)
        cst_b = A.at_f32(40768, 3, 128)
        S.dma("sp", cst_g[0:8, :], norm_g_d.rearrange("(k p) -> k p", p=128), writes=["cst_g"])
        S.dma("sp", cst_m[0:8, :], mem_norm_g_d.rearrange("(k p) -> k p", p=128), writes=["cst_m"])
        S.dma("sp", cst_c[0:12], conv_w_d.rearrange("j (s p) -> s j p", p=128), writes=["cst_c"])
        S.dma("sp", cst_b[0:8], b_gate_d.rearrange("i (k p) -> k i p", p=128), writes=["cst_b"])
        bi = nextbank()
        tr(bank(bi)[:, 0:8], cst_g[0:8, :], ident_f[0:8, 0:8], ["cst_g", "ident_f"], bkeys(bi))
        tr(bank(bi)[:, 8:16], cst_m[0:8, :], ident_f[0:8, 0:8], ["cst_m", "ident_f"], bkeys(bi))
        for j in range(4):
            tr(bank(bi)[:, 16 + 12 * j:28 + 12 * j], cst_c[0:12, j, :], ident_f[0:12, 0:12], ["cst_c", "ident_f"], bkeys(bi))
        for i in range(3):
            tr(bank(bi)[:, 64 + 8 * i:72 + 8 * i], cst_b[0:8, i, :], ident_f[0:8, 0:8], ["cst_b", "ident_f"], bkeys(bi))
        cp("dve", gT, bank(bi)[:, 0:8], bkeys(bi), ["gT"])
        cp("dve", mgT, bank(bi)[:, 8:16], bkeys(bi), ["mgT"])
        cp("dve", cwT.rearrange("p s j -> p j s"), bank(bi)[:, 16:64].rearrange("p (j s) -> p j s", j=4), bkeys(bi), ["cwT"])
        cp("dve", bgT, bank(bi)[:, 64:88].rearrange("p (i k) -> p i k", i=3), bkeys(bi), ["bgT"])
        S.dma("sp", ang[:, 0:1], a_norm_g_d.rearrange("(p o) -> p o", o=1), writes=["ang"], allow_slow_non_contiguous=True)
        S.dma("sp", alB, a_log_d.partition_broadcast(128), writes=["alB"])
        S.dma("sp", dtB, dt_bias_d.partition_broadcast(128), writes=["dtB"])

        def rsqrt_col(dst, src, rows, scale, rk, wk):
            ts("dve", dst, src, scale, EPS, ALU.mult, ALU.add, rk, wk)
            tt("pool", dst, dst, neghalf[0:rows, 0:1], ALU.pow, wk + ["neghalf"], wk)

        def norm_A(src_rows, rows, gcol, dst, dcol, dkey, idx, xt, xs_bf, junk):
            b = idx % len(xt)
            S.dma("sp", xt[b][0:rows, :], src_rows, writes=["xt%d" % b])
            ss = stats[0:rows, idx, 0:1]
            rs = stats[0:rows, idx, 1:2]
            sk = "st%d" % idx
            stt(junk[0:rows, :], xt[b][0:rows, :], 1.0, xt[b][0:rows, :], ALU.mult, ALU.mult,
                ["xt%d" % b, "stats"], ["junk", sk], accum_out=ss)
            rsqrt_col(rs, ss, rows, 1.0 / 1024, [sk], [sk + "r"])
            act(xs_bf[b][0:rows, :], xt[b][0:rows, :], AF.Copy, ["xt%d" % b, sk + "r"], ["xs%d" % b], scale=rs)
            bi = nextbank()
            pT = bank(bi).bitcast(BF16)
            for k in range(8):
                tr(pT[:, k * 128:k * 128 + rows], xs_bf[b][0:rows, k * 128:(k + 1) * 128], ident_b[0:rows, 0:rows],
                   ["xs%d" % b, "ident_b"], bkeys(bi))
            return (bi, pT, rows, gcol, dst, dcol, dkey)

        def norm_B(ctx):
            bi, pT, rows, gcol, dst, dcol, dkey = ctx
            tt("dve", dst[:, :, dcol:dcol + rows],
               pT.rearrange("p (k t) -> p k t", k=8)[:, :, 0:rows],
               bc(gcol.unsqueeze(2), [128, 8, rows]), ALU.mult, bkeys(bi) + ["gT", "mgT"], [dkey])

        def norm_T(*a):
            norm_B(norm_A(*a))

        mark0 = A.off
        A.off = AW - 7000
        xt = [A.f32(1024) for _ in range(4)]
        xs_bf = [A.bf16(1024) for _ in range(4)]
        junk = A.bf16(1024)
        def stage0_gen():
            prev = None
            for n in range(NT + 1):
                if n < NT:
                    ctx = norm_A(x_d[n * 128:(n + 1) * 128, :], 128, gT, hT, n * 128, "hT%d" % n, n, xt, xs_bf, junk)
                else:
                    ctx = norm_A(xs_d, TS, gT, hT, 2048, "hT16", 16, xt, xs_bf, junk)
                if prev is not None:
                    norm_B(prev)
                    if n == NT:
                        cp("dve", hT[:, :, 2064:2067], hT[:, :, 2045:2048], ["hT15"], ["hT16b"])
                prev = ctx
                yield
            norm_B(prev)
            yield

        HS = ["hT16", "hT16b"]

        def hkeys(tb):
            return ["hT%d" % (4 * tb + i) for i in range(4)]

        def load_w(dst, src, key):
            S.dma("pool", dst, src.rearrange("(k p) n -> p k n", p=128), writes=[key])

        def proj_fm(wt, c0, c1, t0, t1, out_ap, rkeys, wkeys, K=8, src=None):
            src = hT if src is None else src
            for k in range(K):
                mm(out_ap, wt[:, k, c0:c1], src[:, k, t0:t1], k == 0, k == K - 1, rkeys, wkeys)

        def proj_tm(wt, c0, c1, t0, t1, out_ap, rkeys, wkeys, K=8, src=None):
            src = hT if src is None else src
            for k in range(K):
                mm(out_ap, src[:, k, t0:t1], wt[:, k, c0:c1], k == 0, k == K - 1, rkeys, wkeys)

        if STOP <= 0:
            S.emit()
            return nc
        A.off = mark0
        wsm = [A.bf16(8, 128) for _ in range(4)]
        w8 = A.bf16(8, 8)
        slab_base = A.off
        slab_pre = [A.f32(2052), A.f32(2052)]
        slab_post = [A.f32(2048), A.f32(2048)]
        slab_end = A.off
        S_all = A.at_f32(slab_base, 64, 128)
        SLABK = ["pre%d_z" % i for i in range(2)] + ["pre%d_%d" % (i, t) for i in range(2) for t in range(4)] + \
                ["post%d_%d" % (i, t) for i in range(2) for t in range(4)]
        sq_bs = [A.bf16(2048), A.bf16(2048)]
        rn4 = [[A.f32(512), A.f32(512)], [A.f32(512), A.f32(512)]]
        sq_b = sq_bs[0]
        rn = rn4[0]
        QT = A.bf16(2048)
        KT = A.bf16(2048)
        VT = A.bf16(2048)
        sgT2 = [A.bf16(2048), A.bf16(2048)]
        kvn = A.bf16(16, 256)
        kd_n = A.bf16(16, 128)
        qdT = A.bf16(16, 128)
        attnT = A.bf16(16, 128)
        NKwT = A.bf16(16, 128)
        u_b = A.bf16(16, 128)
        o_sb = A.f32(2048)
        S_f = A.f32(128)
        S_b2 = [A.bf16(128), A.bf16(128)]
        wu_bf = [A.bf16(256) for _ in range(4)]
        NFL = 4
        dgc = [A.f32(128) for _ in range(NFL)]
        Dup = [A.f32(128) for _ in range(NFL)]
        Dlo = [A.f32(128) for _ in range(NFL)]
        egB = [A.bf16(128) for _ in range(NFL)]
        AA = [[A.bf16(256) for _ in range(2)] for _ in range(NFL)]
        PT = [A.bf16(128) for _ in range(NFL)]
        sB = {nm: sc64[nm] for nm in sc64}

        memset("dve", slab_pre[0][:, 0:3], 0.0, ["pre0_z"])
        memset("dve", slab_pre[1][:, 0:3], 0.0, ["pre1_z"])

        wi = {"i": 0}
        SLAB_BANKS = [5, 6, 7]
        REC_WS = [0, 1]
        REC_SU = [2, 3]
        REC_OT = 4
        sbk = {"i": 0}

        def slab_bank():
            sbk["i"] += 1
            return SLAB_BANKS[sbk["i"] % 3]

        def next_wsm(c0):
            w = wsm[wi["i"] % 4]
            k = "wsm%d" % (wi["i"] % 4)
            wi["i"] += 1
            load_w(w, w_in_d[:, c0:c0 + 128], k)
            return w, k

        slab_ctr = {"i": 0}

        def slab_gen(h, parts, si):
            pre, post = slab_pre[si], slab_post[si]
            sqb, rns = sq_bs[si], rn4[si]
            for part in parts:
                if part == 3:
                    w, wk = next_wsm(OFF["ag"] + h * 128)
                    for tb in range(4):
                        bi = slab_bank()
                        proj_fm(w, 0, 128, tb * 512, (tb + 1) * 512, bank(bi), [wk] + hkeys(tb), bkeys(bi))
                        yield
                        act(sgT2[h % 2][:, tb * 512:(tb + 1) * 512], bank(bi), AF.Silu, bkeys(bi), ["sgT%d_%d" % (h % 2, tb)])
                        yield
                    bi = slab_bank()
                    proj_tm(w, 0, 128, 2048, 2064, bank(bi)[0:TS, 0:128], [wk] + HS, bkeys(bi))
                    yield
                    cp("dve", ags[0:TS, h * 128:(h + 1) * 128], bank(bi)[0:TS, 0:128], bkeys(bi), ["ags%d" % h])
                    yield
                    continue
                c0 = part * 512 + h * 128
                w, wk = next_wsm(c0)
                sl = part * 4 + h
                dst = [QT, KT, VT][part]
                dk_ = ["QT", "KT", "VT"][part]
                for tb in range(4):
                    tsl = slice(tb * 512, (tb + 1) * 512)
                    bi = slab_bank()
                    proj_fm(w, 0, 128, tb * 512, (tb + 1) * 512, bank(bi), [wk] + hkeys(tb), bkeys(bi))
                    yield
                    cp("act", pre[:, 3 + tb * 512:3 + (tb + 1) * 512], bank(bi), bkeys(bi), ["pre%d_%d" % (si, tb)])
                    prk = ["pre%d_%d" % (si, tb), "pre%d_z" % si] + (["pre%d_%d" % (si, tb - 1)] if tb else [])
                    pk_ = "post%d_%d" % (si, tb)
                    act(post[:, tsl], pre[:, tb * 512:tb * 512 + 512], AF.Copy, prk + ["cwT"], [pk_], scale=cwT[:, sl, 0:1])
                    yield
                    for j in range(1, 4):
                        stt(post[:, tsl], pre[:, tb * 512 + j:tb * 512 + j + 512], cwT[:, sl, j:j + 1], post[:, tsl],
                            ALU.mult, ALU.add, prk + ["cwT", pk_], [pk_])
                    yield
                    if part == 2:
                        act(VT[:, tsl], post[:, tsl], AF.Silu, [pk_], ["VT%d" % tb])
                        yield
                    else:
                        act(post[:, tsl], post[:, tsl], AF.Silu, [pk_], [pk_])
                        tt("pool", sqb[:, tsl], post[:, tsl], post[:, tsl], ALU.mult, [pk_], ["sq_b%d_%d" % (si, tb)])
                        yield
                        b2_ = slab_bank()
                        mm(bank(b2_), ones_b, sqb[:, tsl], True, True, ["ones_b", "sq_b%d_%d" % (si, tb)], bkeys(b2_))
                        r = rns[tb % 2]
                        rk = "rn%d_%d" % (si, tb % 2)
                        act(r, bank(b2_), AF.Ln, bkeys(b2_), [rk], bias=EPS)
                        act(r, r, AF.Exp, [rk], [rk], scale=-0.5)
                        yield
                        stt(dst[:, tsl], post[:, tsl], (128.0 ** -0.5) if part == 0 else 1.0, r, ALU.mult, ALU.mult,
                            [pk_, rk], ["%s%d" % (dk_, tb)])
                        yield
                bi = slab_bank()
                proj_tm(w, 0, 128, 2048, 2067, bank(bi)[0:19, 0:128], [wk] + HS, bkeys(bi))
                yield
                cp("dve", xn19[0:19, c0:c0 + 128], bank(bi)[0:19, 0:128], bkeys(bi), ["xn19_%d_%d" % (part, h)])
                yield

        QK4 = lambda nm: ["%s%d" % (nm, t) for t in range(4)]

        def kv_transposes(h):
            beta_h = sB["beta"].rearrange("p (n h) -> p n h", h=4)[:, :, h]
            bge_h = sB["bge"].rearrange("p (n h) -> p n h", h=4)[:, :, h]
            ekd_h = sB["ekd"].rearrange("p (n h) -> p n h", h=4)[:, :, h]
            for half in range(2):
                hs = slice(half * 8, (half + 1) * 8)
                bi = nextbank()
                pT = bank(bi).bitcast(BF16)
                for j in range(8):
                    n = half * 8 + j
                    tr(pT[:, j * 128:(j + 1) * 128], KT[:, n * 128:(n + 1) * 128], ident_b, QK4("KT") + ["ident_b"], bkeys(bi))
                pv = pT.rearrange("p (n d) -> p n d", n=8)
                tt("dve", kvn[:, hs, 0:128], pv, bc(bge_h[:, hs].unsqueeze(2), [128, 8, 128]), ALU.mult, bkeys(bi) + ["sc_bge"], ["kvn_k"])
                tt("dve", kd_n[:, hs, :], pv, bc(ekd_h[:, hs].unsqueeze(2), [128, 8, 128]), ALU.mult, bkeys(bi) + ["sc_ekd"], ["kd_n"])
                bi = nextbank()
                pT = bank(bi).bitcast(BF16)
                for j in range(8):
                    n = half * 8 + j
                    tr(pT[:, j * 128:(j + 1) * 128], VT[:, n * 128:(n + 1) * 128], ident_b, QK4("VT") + ["ident_b"], bkeys(bi))
                pv = pT.rearrange("p (n d) -> p n d", n=8)
                tt("dve", kvn[:, hs, 128:256], pv, bc(beta_h[:, hs].unsqueeze(2), [128, 8, 128]), ALU.mult, bkeys(bi) + ["sc_beta"], ["kvn_v"])

        def precompute(h):
            for g0 in range(0, NT, NFL):
                units = [(n, n - g0) for n in range(g0, g0 + NFL)]
                bA = lambda f: 2 * f
                bB = lambda f: 2 * f + 1
                qA = lambda f, i: bank(bA(f))[:, i * 128:(i + 1) * 128]
                kA = lambda f: bkeys(bA(f))
                kB = lambda f: bkeys(bB(f))
                for n, f in units:
                    c = n * 4 + h
                    tsl = slice(n * 128, (n + 1) * 128)
                    ts("pool", dgc[f], ident_f, sB["gc"][:, c:c + 1], None, ALU.mult, None, ["ident_f", "sc_gc"], ["dgc%d" % f])
                    mm(qA(f, 0), ones_f, dgc[f], True, True, ["ones_f", "dgc%d" % f], kA(f))
                    mm(qA(f, 1), KT[:, tsl], KT[:, tsl], True, True, QK4("KT"), kA(f))
                    mm(qA(f, 2), KT[:, tsl], QT[:, tsl], True, True, QK4("KT") + QK4("QT"), kA(f))
                for n, f in units:
                    c = n * 4 + h
                    gcc = sB["gc"][:, c:c + 1]
                    stt(Dup[f], qA(f, 0), gcc, NEG_up, ALU.subtract, ALU.add, kA(f) + ["sc_gc", "NEG_up"], ["Dup%d" % f])
                    stt(Dlo[f], qA(f, 0), gcc, POS_lo, ALU.subtract, ALU.add, kA(f) + ["sc_gc", "POS_lo"], ["Dlo%d" % f])
                    act(egB[f], qA(f, 0), AF.Exp, kA(f), ["egB%d" % f])
                    act(Dup[f], Dup[f], AF.Exp, ["Dup%d" % f], ["Dup%d" % f])
                    act(Dlo[f], Dlo[f], AF.Exp, ["Dlo%d" % f], ["Dlo%d" % f], scale=-1.0)
                for n, f in units:
                    c = n * 4 + h
                    tsl = slice(n * 128, (n + 1) * 128)
                    tt("pool", qdT[:, n, :], QT[:, tsl], egB[f], ALU.mult, QK4("QT") + ["egB%d" % f], ["qdT%d" % n])
                    stt(AA[f][0][:, 0:128], qA(f, 1), sB["nbeta"][:, c:c + 1], Dlo[f], ALU.mult, ALU.mult,
                        kA(f) + ["sc_nbeta", "Dlo%d" % f], ["AA%d_0" % f])
                    tt("dve", attnT[:, n, :], qA(f, 2), Dup[f], ALU.mult, kA(f) + ["Dup%d" % f], ["attnT%d" % n])
                for n, f in units:
                    q3b = bank(bB(f)).bitcast(BF16)[:, 0:128]
                    tr(q3b, AA[f][0][:, 0:128], ident_b, ["AA%d_0" % f, "ident_b"], kB(f))
                Mo32T = lambda f: Dup[f].bitcast(BF16)[:, 0:128]
                Mo64 = lambda f: Dup[f].bitcast(BF16)[:, 128:256]
                P_ = lambda f: Dlo[f].bitcast(BF16)[:, 0:128]
                U_ = lambda f: Dlo[f].bitcast(BF16)[:, 128:256]
                for n, f in units:
                    q3b = bank(bB(f)).bitcast(BF16)[:, 0:128]
                    cp("act", egB[f], q3b, kB(f) + ["qdT%d" % n], ["egB%d" % f])
                    tt("dve", AA[f][0][:, 128:256], q3b, BD32, ALU.mult, kB(f) + ["BD32"], ["AA%d_0T" % f])
                    tt("dve", PT[f], AA[f][0][:, 128:256], ident_b, ALU.add, ["AA%d_0T" % f, "ident_b"], ["PT%d" % f])
                for n, f in units:
                    tt("pool", Mo64(f), AA[f][0][:, 0:128], M64o, ALU.mult, ["AA%d_0" % f, "M64o", "attnT%d" % n], ["Dup%d" % f])
                    tt("pool", AA[f][0][:, 0:128], AA[f][0][:, 0:128], BD32, ALU.mult, ["AA%d_0" % f, "BD32", "Dup%d" % f], ["AA%d_0" % f])
                for n, f in units:
                    tt("pool", Mo32T(f), egB[f], M32o, ALU.mult, ["egB%d" % f, "M32o", "Dup%d" % f], ["Dup%d" % f])
                for lvl in range(1, 5):
                    sk_ = lambda f: "AA%d_%d" % (f, (lvl - 1) % 2)
                    dk2 = lambda f: "AA%d_%d" % (f, lvl % 2)
                    for n, f in units:
                        src_ = AA[f][(lvl - 1) % 2]
                        mm(qA(f, 0), src_[:, 128:256], src_[:, 0:128], True, True, [sk_(f), sk_(f) + "T"], kA(f))
                        if lvl < 4:
                            mm(qA(f, 1), src_[:, 0:128], src_[:, 128:256], True, True, [sk_(f), sk_(f) + "T"], kA(f))
                    for n, f in units:
                        dst_ = AA[f][lvl % 2]
                        if lvl < 4:
                            cp(evac_eng(), dst_, bank(bA(f))[:, 0:256], kA(f), [dk2(f), dk2(f) + "T"])
                        else:
                            cp(evac_eng(), dst_[:, 0:128], qA(f, 0), kA(f), [dk2(f)])
                    for n, f in units:
                        dst_ = AA[f][lvl % 2]
                        mm(bank(bB(f))[:, 0:128], ident_b, PT[f], True, False, ["ident_b", "PT%d" % f], kB(f))
                        mm(bank(bB(f))[:, 0:128], dst_[:, 0:128], PT[f], False, True, [dk2(f), "PT%d" % f], kB(f))
                    for n, f in units:
                        cp(evac_eng(), PT[f], bank(bB(f))[:, 0:128], kB(f), ["PT%d" % f])
                for n, f in units:
                    trb = bank(bB(f)).bitcast(BF16)[:, 0:128]
                    tr(trb, PT[f], ident_b, ["PT%d" % f, "ident_b"], kB(f))
                for n, f in units:
                    trb = bank(bB(f)).bitcast(BF16)[:, 0:128]
                    cp(evac_eng(), P_(f), trb, kB(f) + ["AA%d_0" % f], ["Dlo%d" % f])
                for n, f in units:
                    mm(qA(f, 0), Mo32T(f), P_(f), True, True, ["Dup%d" % f, "Dlo%d" % f], kA(f))
                for n, f in units:
                    cp(evac_eng(), U_(f), qA(f, 0), kA(f), ["Dlo%d" % f])
                for n, f in units:
                    mm(bank(bB(f))[:, 0:128], ident_b, P_(f), True, False, ["ident_b", "Dlo%d" % f], kB(f))
                    mm(bank(bB(f))[:, 0:128], PT[f], U_(f), False, True, ["PT%d" % f, "Dlo%d" % f], kB(f))
                for n, f in units:
                    cp(evac_eng(), P_(f), bank(bB(f))[:, 0:128], kB(f), ["Dlo%d" % f])
                for n, f in units:
                    trb = bank(bA(f)).bitcast(BF16)[:, 0:128]
                    tr(trb, P_(f), ident_b, ["Dlo%d" % f, "ident_b"], kA(f))
                for n, f in units:
                    trb = bank(bA(f)).bitcast(BF16)[:, 0:128]
                    cp(evac_eng(), PT[f], trb, kA(f), ["PT%d" % f])
                for n, f in units:
                    mm(bank(bB(f))[:, 0:128], Mo64(f), PT[f], True, True, ["Dup%d" % f, "PT%d" % f], kB(f))
                for n, f in units:
                    cp(evac_eng(), U_(f), bank(bB(f))[:, 0:128], kB(f), ["Dlo%d" % f])
                for n, f in units:
                    mm(qA(f, 0), ident_b, PT[f], True, False, ["ident_b", "PT%d" % f], kA(f))
                    mm(qA(f, 0), P_(f), U_(f), False, True, ["Dlo%d" % f], kA(f))
                for n, f in units:
                    cp(evac_eng(), PT[f], qA(f, 0), kA(f), ["PT%d" % f])
                for n, f in units:
                    mm(bank(bA(f))[:, 0:256], PT[f], kvn[:, n, :], True, True, ["kvn_k", "kvn_v", "PT%d" % f], kA(f))
                for n, f in units:
                    cp("act", wu_bf[f][:, 0:128], bank(bA(f))[:, 0:128], kA(f), ["wu%d" % f])
                    cp("dve", u_b[:, n, :], bank(bA(f))[:, 128:256], kA(f), ["u%d" % n])
                for n, f in units:
                    mm(bank(bB(f))[:, 0:128], wu_bf[f][:, 0:128], kd_n[:, n, :], True, True, ["wu%d" % f, "kd_n"], kB(f))
                    mm(bank(bB(f))[:, 128:256], wu_bf[f][:, 0:128], attnT[:, n, :], True, True, ["wu%d" % f, "attnT%d" % n], kB(f))
                for n, f in units:
                    act(NKwT[:, n, :], bank(bB(f))[:, 0:128], AF.Copy, kB(f), ["NKwT%d" % n], scale=-1.0)
                    tt("dve", qdT[:, n, :], qdT[:, n, :], bank(bB(f))[:, 128:256], ALU.subtract, kB(f) + ["qdT%d" % n], ["qdT%d" % n])

        def recurrence_gen(h):
            memset("dve", S_f, 0.0, ["S_f"])
            for n in range(NT):
                c = n * 4 + h
                b2, b3 = REC_SU[n % 2], REC_OT
                Sp, Spk = S_b2[(n + 1) % 2], "S_b%d" % ((n + 1) % 2)
                Sn, Snk = S_b2[n % 2], "S_b%d" % (n % 2)
                mm(bank(b2)[:, 0:128], kd_n[:, n, :], u_b[:, n, :], True, n == 0, ["kd_n", "u%d" % n], bkeys(b2))
                if n > 0:
                    mm(bank(b2)[:, 0:128], NKwT[:, n, :], Sp, False, True, ["NKwT%d" % n, Spk], bkeys(b2))
                mm(bank(b3)[:, 0:128], u_b[:, n, :], attnT[:, n, :], True, n == 0, ["u%d" % n, "attnT%d" % n], bkeys(b3))
                if n > 0:
                    mm(bank(b3)[:, 0:128], Sp, qdT[:, n, :], False, True, [Spk, "qdT%d" % n], bkeys(b3))
                egl_c = sB["egl"][:, c:c + 1]
                stt(Sn, S_f, egl_c, bank(b2)[:, 0:128], ALU.mult, ALU.add, ["S_f", "sc_egl"] + bkeys(b2), [Snk])
                stt(S_f, S_f, egl_c, bank(b2)[:, 0:128], ALU.mult, ALU.add, ["S_f", "sc_egl"] + bkeys(b2), ["S_f"])
                cp("act", o_sb[:, n * 128:(n + 1) * 128], bank(b3)[:, 0:128], bkeys(b3), ["o_sb%d" % (n // 4)])
                yield
            S.dma("sp", sdp_d[h], S_f, reads=["S_f"])

        def outnorm(h):
            for tb in range(4):
                tsl = slice(tb * 512, (tb + 1) * 512)
                tt("pool", sq_b[:, tsl], o_sb[:, tsl], o_sb[:, tsl], ALU.mult, ["o_sb%d" % tb], ["sq_b0_%d" % tb])
                bi = slab_bank()
                mm(bank(bi), ones_b, sq_b[:, tsl], True, True, ["ones_b", "sq_b0_%d" % tb], bkeys(bi))
                r = rn[tb % 2]
                rk = "rn0_%d" % (tb % 2)
                act(r, bank(bi), AF.Ln, bkeys(bi), [rk], bias=EPS, scale=1.0 / 128)
                act(r, r, AF.Exp, [rk], [rk], scale=-0.5)
                stt(r, o_sb[:, tsl], ang[:, 0:1], r, ALU.mult, ALU.mult, ["o_sb%d" % tb, "ang", rk], [rk])
                tt("dve", yaT[:, h, tsl], r, sgT2[h % 2][:, tsl], ALU.mult, [rk, "sgT%d_%d" % (h % 2, tb)], ["yaT"])

        def drain(*gens, weights=None):
            gens = [g for g in gens if g is not None]
            wts = {id(g): (weights[i] if weights else 1) for i, g in enumerate(gens)}
            while gens:
                for g in list(gens):
                    for _ in range(wts[id(g)]):
                        try:
                            next(g)
                        except StopIteration:
                            gens.remove(g)
                            break

        g0 = stage0_gen()
        rr["pool"] = [0, 1, 2, 3, 4]
        for _ in range(int(os.environ.get("MK_PRE", "12"))):
            next(g0)
        drain(g0, slab_gen(0, [0, 2], 0), slab_gen(0, [1, 3], 1), weights=[1, 2, 2])
        rr["pool"] = list(range(8))
        load_w(w8, w_in_d[:, OFF["ab"]:OFF["ab"] + 8], "w8")
        bi = nextbank()
        for n in range(NT):
            proj_tm(w8, 0, 8, n * 128, (n + 1) * 128, bank(bi)[:, n * 8:(n + 1) * 8], ["w8", "hT%d" % n], bkeys(bi, 0, 1))
        cp("dve", ba.rearrange("p n c -> p (n c)"), bank(bi)[:, 0:128], bkeys(bi, 0, 1), ["ba"])
        bi = nextbank()
        proj_tm(w8, 0, 8, 2048, 2064, bank(bi)[0:TS, 0:8], ["w8"] + HS, bkeys(bi, 0, 1))
        cp("dve", bas[0:TS, :], bank(bi)[0:TS, 0:8], bkeys(bi, 0, 1), ["bas"])

        def scalars(rows, logits, nn, out, px=""):
            v = lambda ap: ap[0:rows, 0:nn * 4].rearrange("p (n h) -> p n h", h=4)
            beta, g_, ta, tb_ = v(out["beta"]), v(out["g"]), v(out["tmpa"]), v(out["tmpb"])
            logits = logits[0:rows]
            act(beta, logits[:, :, 0:4], AF.Sigmoid, ["ba", "bas"], [px + "sc_beta"])
            tt("dve", ta, logits[:, :, 4:8], bc(dtB[0:rows, :].unsqueeze(1), [rows, nn, 4]), ALU.add,
               ["ba", "bas", "dtB"], [px + "sc_ta"])
            stt(tb_, ta, -1.0, ta, ALU.mult, ALU.max, [px + "sc_ta"], [px + "sc_tb"])
            act(tb_, tb_, AF.Exp, [px + "sc_tb"], [px + "sc_tb"], scale=-1.0)
            act(tb_, tb_, AF.Ln, [px + "sc_tb"], [px + "sc_tb"], bias=1.0)
            stt(ta, ta, 0.0, tb_, ALU.max, ALU.add, [px + "sc_ta", px + "sc_tb"], [px + "sc_ta"])
            act(tb_[:, 0, :], alB[0:rows, :], AF.Exp, ["alB", px + "sc_tb"], [px + "sc_tb"])
            stt(g_, ta, -1.0, bc(tb_[:, 0:1, :], [rows, nn, 4]), ALU.mult, ALU.mult, [px + "sc_ta", px + "sc_tb"], [px + "sc_g"])

        scalars(128, ba, NT, sB)
        bi = nextbank()
        mm(bank(bi)[:, 0:64], U_f, sB["g"], True, True, ["U_f", "sc_g"], bkeys(bi))
        cp("dve", sB["gc"], bank(bi)[:, 0:64], bkeys(bi), ["sc_gc"])
        ts("dve", sB["ngc"], sB["gc"], -1.0, None, ALU.mult, None, ["sc_gc"], ["sc_ngc"])
        bi = nextbank()
        mm(bank(bi)[:, 0:64], ones_f, sB["g"], True, True, ["ones_f", "sc_g"], bkeys(bi))
        cp("dve", sB["gl"], bank(bi)[:, 0:64], bkeys(bi), ["sc_gl"])
        act(sB["egl"], sB["gl"], AF.Exp, ["sc_gl"], ["sc_egl"])
        tt("dve", sB["ekd"], sB["gl"], sB["gc"], ALU.subtract, ["sc_gl", "sc_gc"], ["sc_ekd"])
        act(sB["ekd"], sB["ekd"], AF.Exp, ["sc_ekd"], ["sc_ekd"])
        act(sB["bge"], sB["gc"], AF.Exp, ["sc_gc"], ["sc_bge"])
        tt("dve", sB["bge"], sB["bge"], sB["beta"], ALU.mult, ["sc_bge", "sc_beta"], ["sc_bge"])
        ts("dve", sB["nbeta"], sB["beta"], -1.0, None, ALU.mult, None, ["sc_beta"], ["sc_nbeta"])
        scalars(TS, bas.rearrange("p (n c) -> p n c", n=1), 1, sS, "s")

        for h in range(4):
            if h == 0:
                assert slab_end + 2 * 2048 + 2 * 2048 + 3 * 1024 + 2 * 1024 <= AW - 7000
                S.barrier()
            kv_transposes(h)
            if h == 3:
                for h_ in range(4):
                    for s4 in range(4):
                        S.dma("sp", S_all[:, 16 * h_ + 4 * s4:16 * h_ + 4 * s4 + 4, :],
                              sd_d[4 * s4:4 * s4 + 4, h_].rearrange("s k v -> k s v"), writes=SLABK + ["S_all%d" % h_])
            precompute(h)
            if h < 3:
                drain(recurrence_gen(h), slab_gen(h + 1, [0, 2], 0), slab_gen(h + 1, [1, 3], 1), weights=[1, 2, 2])
            else:
                drain(recurrence_gen(h))
            outnorm(h)

        S.dma("sp", scp_d, xn19[16:19, :], reads=["xn19_%d_%d" % (p_, h_) for p_ in range(3) for h_ in range(4)])

        if STOP <= 1:
            S.emit()
            return nc
        S.barrier()
        A.off = slab_end
        pm = A.f32(516)
        markA2 = A.off
        cwB = A.f32(4, 1536)
        scs_sb = A.f32(4, 1536)
        prod = cwB
        qkv = A.f32(1536)
        pk = A.f32(4, 514)
        r8 = A.f32(8)
        XN = ["xn19_%d_%d" % (p_, h_) for p_ in range(3) for h_ in range(4)]
        S.dma("act", cwB[0:TS].rearrange("p j c -> p (j c)"), conv_w_d.rearrange("j c -> (j c)").partition_broadcast(TS), writes=["cwB"])
        S.dma("sp", scs_sb[0:TS, 0:3, :], sc_d, writes=["scs03"])
        cp("dve", scs_sb[0:TS, 3, :], xn19[0:TS, :], XN, ["scs3"])
        S.dma("sp", scs_d, scs_sb[0:TS, 1:4, :], reads=["scs03", "scs3"])
        tt("dve", prod[0:TS], scs_sb[0:TS], cwB[0:TS], ALU.mult, ["scs03", "scs3", "cwB"], ["prod", "cwB"])
        tt("dve", qkv[0:TS], prod[0:TS, 0, :], prod[0:TS, 1, :], ALU.add, ["prod"], ["qkv"])
        tt("dve", qkv[0:TS], qkv[0:TS], prod[0:TS, 2, :], ALU.add, ["prod", "qkv"], ["qkv"])
        tt("dve", qkv[0:TS], qkv[0:TS], prod[0:TS, 3, :], ALU.add, ["prod", "qkv"], ["qkv"])
        act(qkv[0:TS], qkv[0:TS], AF.Silu, ["qkv"], ["qkv"])
        sqv = prod[0:TS, 0, 0:1024]
        tt("dve", sqv, qkv[0:TS, 0:1024], qkv[0:TS, 0:1024], ALU.mult, ["qkv", "prod"], ["prod"])
        S.op("dve", lambda e: e.tensor_reduce(out=r8[0:TS, :], in_=sqv.rearrange("p (a b) -> p a b", b=128), axis=AX.X,
                                              op=ALU.add), ["prod"], ["r8"])
        ts("dve", r8[0:TS, :], r8[0:TS, :], 1.0, EPS, ALU.mult, ALU.add, ["r8"], ["r8"])
        tt("pool", r8[0:TS, :], r8[0:TS, :], bc(neghalf[0:TS, 0:1], [TS, 8]), ALU.pow, ["r8", "neghalf"], ["r8"])
        ts("dve", r8[0:TS, 0:4], r8[0:TS, 0:4], 128.0 ** -0.5, None, ALU.mult, None, ["r8"], ["r8"])
        q3 = qkv[0:TS, :].rearrange("p (a h d) -> p a h d", a=3, h=4)
        tt("dve", pk[0:TS, :, 0:128], q3[:, 1], bc(r8[0:TS, 4:8].unsqueeze(2), [TS, 4, 128]), ALU.mult, ["qkv", "r8"], ["pk"])
        tt("dve", pk[0:TS, :, 128:256], q3[:, 0], bc(r8[0:TS, 0:4].unsqueeze(2), [TS, 4, 128]), ALU.mult, ["qkv", "r8", "pk"], ["pk"])
        cp("dve", pk[0:TS, :, 256:384], q3[:, 2], ["qkv", "pk"], ["pk"])
        act(pk[0:TS, :, 384:512], ags[0:TS, :].rearrange("p (h d) -> p h d", h=4), AF.Silu,
            ["ags%d" % h_ for h_ in range(4)] + ["pk"], ["pk"])
        cp("dve", pk[0:TS, :, 512:513], sS["beta"][0:TS, :].unsqueeze(2), ["ssc_beta", "pk"], ["pk"])
        act(pk[0:TS, :, 513:514], sS["g"][0:TS, :].unsqueeze(2), AF.Exp, ["ssc_g", "pk"], ["pk"])
        for h in range(4):
            S.dma("sp" if h % 2 == 0 else "act", pm[16 * h:16 * h + 16, 0:514], pk[0:TS, h, :], reads=["pk"], writes=["pm%d" % h])
        PMK = ["pm%d" % h for h in range(4)]
        if STOP <= 2:
            S.emit()
            return nc
        S.barrier()
        A.off = markA2
        KQm = A.f32(64, 128)
        Kmask = A.f32(64, 128)
        SkSq = A.f32(128)
        Sq_pm = A.f32(128)
        wk1 = A.f32(128)
        wk2 = A.f32(128)
        vnew_pm = A.f32(128)
        o_pm = A.f32(128)
        angB = A.f32(128)
        sml = A.f32(8)
        aB = A.f32(64)
        dga = A.f32(64)
        jk = A.f32(128)
        SA = ["S_all%d" % h for h in range(4)]
        S.dma("sp", angB[0:64, :], a_norm_g_d.partition_broadcast(64), writes=["angB"])
        memset("pool", KQm, 0.0, ["KQm"])
        bi = nextbank()
        tr(bank(bi)[:, 0:64], pm[0:64, 0:128], ident_f[0:64, 0:64], PMK + ["ident_f"], bkeys(bi, 0, 1))
        tr(bank(bi)[:, 128:192], pm[0:64, 128:256], ident_f[0:64, 0:64], PMK + ["ident_f"], bkeys(bi, 1, 2))
        cp("dve", custom(KQm, 0, [[129, 64]]), bank(bi)[:, 0:64], bkeys(bi, 0, 1) + ["KQm"], ["KQm"])
        cp("dve", custom(KQm, 64, [[129, 64]]), bank(bi)[:, 128:192], bkeys(bi, 1, 2) + ["KQm"], ["KQm"])
        bi = nextbank()
        for p in range(64):
            mm(bank(bi)[:, 0:128], KQm[:, p, :], S_all[:, p, :], p == 0, p == 63, ["KQm"] + SA, bkeys(bi, 0, 1))
        a_c = pm[0:64, 513:514]
        memset("pool", Kmask, 0.0, ["Kmask"])
        memset("pool", dga, 0.0, ["dga"])
        cp("dve", Kmask[0:64], bc(pm[0:64, 0:128].unsqueeze(1), [64, 64, 128]), PMK + ["Kmask"], ["Kmask"])
        tt("dve", Kmask[0:64], Kmask[0:64], bc(ident_f[0:64, 0:64].unsqueeze(2), [64, 64, 128]), ALU.mult,
           ["Kmask", "ident_f"], ["Kmask"])
        ts("dve", dga[0:64, :], ident_f[0:64, 0:64], a_c, None, ALU.mult, None, ["ident_f", "dga"] + PMK, ["dga"])
        bi_a = nextbank()
        mm(bank(bi_a)[:, 0:64], ones_f, dga, True, True, ["ones_f", "dga"], bkeys(bi_a, 0, 1))
        cp("dve", aB, bank(bi_a)[:, 0:64], bkeys(bi_a, 0, 1), ["aB"])
        cp("dve", SkSq, bank(bi)[:, 0:128], bkeys(bi, 0, 1), ["SkSq"])
        S.dma("sp", Sq_pm[0:64, :], SkSq[64:128, :], reads=["SkSq"], writes=["Sq_pm"])
        if STOP <= 2.3:
            S.emit()
            return nc
        a_c = pm[0:64, 513:514]
        beta_c = pm[0:64, 512:513]
        ts("dve", sml[0:64, 0:1], beta_c, -1.0, None, ALU.mult, None, PMK, ["sml0"])
        stt(wk1[0:64, :], SkSq[0:64, :], a_c, pm[0:64, 256:384], ALU.mult, ALU.subtract, ["SkSq"] + PMK, ["wk1"])
        memset("pool", vnew_pm, 0.0, ["vnew_pm"])
        ts("dve", vnew_pm[0:64, :], wk1[0:64, :], sml[0:64, 0:1], None, ALU.mult, None, ["wk1", "sml0", "vnew_pm"], ["vnew_pm"])
        stt(jk[0:64, :], pm[0:64, 0:128], 1.0, pm[0:64, 128:256], ALU.mult, ALU.mult, PMK, ["jk", "sml1"],
            accum_out=sml[0:64, 1:2])
        ts("dve", wk2[0:64, :], Sq_pm[0:64, :], a_c, None, ALU.mult, None, ["Sq_pm"] + PMK, ["wk2"])
        stt(o_pm[0:64, :], vnew_pm[0:64, :], sml[0:64, 1:2], wk2[0:64, :], ALU.mult, ALU.add, ["vnew_pm", "sml1", "wk2"], ["o_pm"])
        stt(jk[0:64, :], o_pm[0:64, :], 1.0, o_pm[0:64, :], ALU.mult, ALU.mult, ["o_pm", "jk"], ["jk", "sml2"],
            accum_out=sml[0:64, 2:3])
        rsqrt_col(sml[0:64, 3:4], sml[0:64, 2:3], 64, 1.0 / 128, ["sml2"], ["sml3"])
        stt(wk1[0:64, :], o_pm[0:64, :], sml[0:64, 3:4], angB[0:64, :], ALU.mult, ALU.mult, ["o_pm", "sml3", "angB", "wk1"], ["wk1"])
        tt("dve", wk1[0:64, :], wk1[0:64, :], pm[0:64, 384:512], ALU.mult, ["wk1"] + PMK, ["wk1"])
        bi = nextbank()
        tr(bank(bi)[:, 0:64], wk1[0:64, :], ident_f[0:64, 0:64], ["wk1", "ident_f"], bkeys(bi, 0, 1))
        cp("dve", yaT[:, :, 2048:2064], bank(bi)[:, 0:64].rearrange("p (h s) -> p h s", h=4), bkeys(bi, 0, 1), ["yaT_s"])
        if STOP <= 2.6:
            S.emit()
            return nc
        if STOP <= 2.8:
            S.emit()
            return nc
        for p in range(64):
            bi = nextbank()
            qq = 0
            mm(bank(bi)[:, qq * 128:(qq + 1) * 128], Kmask[:, p, :], vnew_pm, True, True, ["Kmask", "vnew_pm"],
               bkeys(bi, qq, qq + 1))
            stt(S_all[:, p, :], S_all[:, p, :], aB[:, p:p + 1], bank(bi)[:, qq * 128:(qq + 1) * 128], ALU.mult, ALU.add,
                SA + ["aB"] + bkeys(bi, qq, qq + 1), ["Snew%d" % p])
        if STOP <= 2.9:
            S.emit()
            return nc
        for h in range(4):
            for s4 in range(4):
                S.dma("sp" if (h + s4) % 2 == 0 else "act", sds_d[4 * s4:4 * s4 + 4, h].rearrange("s k v -> k s v"),
                      S_all[:, 16 * h + 4 * s4:16 * h + 4 * s4 + 4, :],
                      reads=["Snew%d" % p for p in range(16 * h + 4 * s4, 16 * h + 4 * s4 + 4)])

        if STOP <= 3:
            S.emit()
            return nc
        S.barrier()
        A.off = mark0
        A.top = TOP_YC
        cgT = A.bf16(4, 2064)
        qsT = A.f32(64)
        Ks = [A.f32(2, 512) for _ in range(4)]

        def k_load(s_):
            S.dma("sp", Ks[s_ % 4], cmk_d[s_].rearrange("(mt m) c -> m mt c", m=128), writes=["Ks%d" % (s_ % 4)])

        for s_ in range(4):
            k_load(s_)
        markC1 = A.off
        wC = [A.bf16(8, 512) for _ in range(2)]
        xt = [A.f32(1024), A.f32(1024)]
        xs_bf = [A.bf16(1024), A.bf16(1024)]
        junk = A.bf16(1024)
        memnT = A.bf16(8, 256)
        mkT_b = A.bf16(4, 256)
        Vm_b = A.bf16(2, 512)
        mo_f = [A.f32(512), A.f32(512)]
        cqT = A.bf16(4, 2064)
        Pb = [A.bf16(512) for _ in range(4)]
        rden = [A.f32(512), A.f32(512)]
        ot = [A.f32(512), A.f32(512)]
        for mt in range(2):
            norm_T(mem_d[mt * 128:(mt + 1) * 128, :], 128, mgT, memnT, mt * 128, "memnT", 17 + mt, xt, xs_bf, junk)
        load_w(wC[0], w_mkv_d[:, 0:512], "wC0")
        load_w(wC[1], w_mkv_d[:, 512:1024], "wC1")
        for kv in range(2):
            for mt in range(2):
                bi = nextbank()
                proj_tm(wC[kv], 0, 512, mt * 128, (mt + 1) * 128, bank(bi), ["wC%d" % kv, "memnT"], bkeys(bi), src=memnT)
                mo = mo_f[mt]
                cp("act", mo, bank(bi), bkeys(bi), ["mo%d" % mt])
                S.dma("sp", (mkp_d if kv == 0 else mvp_d)[mt * 128:(mt + 1) * 128, :], mo, reads=["mo%d" % mt])
                if kv == 1:
                    cp("dve", Vm_b[:, mt, :], bank(bi), bkeys(bi), ["Vm_b"])
        for h in range(4):
            bi = nextbank()
            proj_fm(wC[0], h * 128, (h + 1) * 128, 0, 256, bank(bi)[:, 0:256], ["wC0", "memnT"], bkeys(bi, 0, 2), src=memnT)
            cp("dve", mkT_b[:, h, :], bank(bi)[:, 0:256], bkeys(bi, 0, 2), ["mkT_b"])
        load_w(wC[0], w_in_d[:, OFF["cq"]:OFF["cq"] + 512], "wC0")
        load_w(wC[1], w_in_d[:, OFF["cg"]:OFF["cg"] + 512], "wC1")
        TBL = [(tb * 512, (tb + 1) * 512, hkeys(tb)) for tb in range(4)] + [(2048, 2064, HS)]
        for h in range(4):
            for (t0, t1, hk) in TBL:
                w_ = t1 - t0
                bi = nextbank()
                proj_fm(wC[0], h * 128, (h + 1) * 128, t0, t1, bank(bi)[:, 0:w_], ["wC0"] + hk, bkeys(bi))
                act(cqT[:, h, t0:t1], bank(bi)[:, 0:w_], AF.Copy, bkeys(bi), ["cqT"], scale=128.0 ** -0.5)
                if t0 == 2048:
                    act(qsT[:, h * 16:(h + 1) * 16], bank(bi)[:, 0:w_], AF.Copy, bkeys(bi), ["qsT"], scale=128.0 ** -0.5)
                bi = nextbank()
                proj_fm(wC[1], h * 128, (h + 1) * 128, t0, t1, bank(bi)[:, 0:w_], ["wC1"] + hk, bkeys(bi))
                act(cgT[:, h, t0:t1], bank(bi)[:, 0:w_], AF.Silu, bkeys(bi), ["cgT"])
        for h in range(4):
            for tb in range(4):
                tsl = slice(tb * 512, (tb + 1) * 512)
                pbs = []
                for mt in range(2):
                    bi = nextbank()
                    mm(bank(bi), mkT_b[:, h, mt * 128:(mt + 1) * 128], cqT[:, h, tsl], True, True, ["mkT_b", "cqT"], bkeys(bi))
                    pi = (tb % 2) * 2 + mt
                    act(Pb[pi], bank(bi), AF.Exp, bkeys(bi), ["Pb%d" % pi])
                    pbs.append(pi)
                bo = nextbank()
                bd = nextbank()
                for mt in range(2):
                    mm(bank(bo), Vm_b[:, mt, h * 128:(h + 1) * 128], Pb[pbs[mt]], mt == 0, mt == 1, ["Vm_b", "Pb%d" % pbs[mt]], bkeys(bo))
                for mt in range(2):
                    mm(bank(bd), ones_b, Pb[pbs[mt]], mt == 0, mt == 1, ["ones_b", "Pb%d" % pbs[mt]], bkeys(bd))
                rd = rden[tb % 2]
                rk = "rden%d" % (tb % 2)
                act(rd, bank(bd), AF.Ln, bkeys(bd), [rk])
                act(rd, rd, AF.Exp, [rk], [rk], scale=-1.0)
                o_ = ot[tb % 2]
                ok = "ot%d" % (tb % 2)
                tt("dve", o_, bank(bo), rd, ALU.mult, bkeys(bo) + [rk], [ok])
                tt("pool", ycT[:, h, tsl], o_, cgT[:, h, tsl], ALU.mult, [ok, "cgT"], ["ycT"])
        if STOP <= 4:
            S.emit()
            return nc
        S.barrier()
        A.off = markC1
        Qm = A.f32(64, 64)
        Vs = [A.f32(2, 512) for _ in range(4)]

        def v_load(s_):
            S.dma("act", Vs[s_ % 4], cmv_d[s_].rearrange("(mt m) c -> m mt c", m=128), writes=["Vs%d" % (s_ % 4)])

        for s_ in range(4):
            v_load(s_)
        KTs = [A.f32(4, 256), A.f32(4, 256)]
        Pf = A.f32(256)
        PTm = A.f32(2, 16 * 64)
        selm = A.f32(16 * 64)
        hm = A.f32(4)
        smx = A.f32(8)
        opm = A.f32(128)
        memset("pool", Qm, 0.0, ["Qm"])
        cp("dve", custom(Qm, 0, [[65, 64]]), qsT, ["qsT", "Qm"], ["Qm"])
        memset("pool", selm, 1.0, ["selm"])
        asel(selm, selm, [[1, 16], [0, 4], [-1, 16]], ALU.is_equal, 0.0, 0, 0, ["selm"], ["selm"])
        memset("pool", hm[0:64, :], 1.0, ["hm"])
        asel(hm[0:64, :], hm[0:64, :], [[-16, 4]], ALU.is_ge, 0.0, 0, 1, ["hm"], ["hm"])
        asel(hm[0:64, :], hm[0:64, :], [[16, 4]], ALU.is_ge, 0.0, 15, -1, ["hm"], ["hm"])
        bsc = 7
        cnt = 0
        def katt_T(s):
            kb = Ks[s % 4]
            kk = "Ks%d" % (s % 4)
            if s >= 1 and s + 3 < TS:
                k_load(s + 3)
            b1, b2 = (5, 6) if s % 2 == 0 else (3, 4)
            for h in range(4):
                for mt in range(2):
                    bb = b1 if h < 2 else b2
                    off = (h % 2) * 256 + mt * 128
                    tr(bank(bb)[:, off:off + 128], kb[:, mt, h * 128:(h + 1) * 128], ident_f, [kk, "ident_f"], bkeys(bb))
            kt = KTs[s % 2]
            ktk = "KTs%d" % (s % 2)
            cp("act", kt[:, 0:2, :].rearrange("p a b -> p (a b)"), bank(b1), bkeys(b1), [ktk + "a"])
            cp("dve", kt[:, 2:4, :].rearrange("p a b -> p (a b)"), bank(b2), bkeys(b2), [ktk + "b"])

        def katt_M(s):
            kt = KTs[s % 2]
            ktk = "KTs%d" % (s % 2)
            for h in range(4):
                p = h * 16 + s
                c_ = s * 4 + h
                mm(bank(bsc)[0:64, 0:256], Qm[:, p, :], kt[:, h, :], c_ == 0, c_ == 63, ["Qm", ktk + ("a" if h < 2 else "b")], bkeys(bsc))

        katt_T(0)
        for s in range(TS):
            if s + 1 < TS:
                katt_T(s + 1)
            katt_M(s)
        S.op("dve", lambda e: e.tensor_reduce(out=smx[0:64, 0:1], in_=bank(bsc)[0:64, 0:256], axis=AX.X, op=ALU.max),
             bkeys(bsc), ["smx0"])
        ts("dve", smx[0:64, 1:2], smx[0:64, 0:1], -1.0, None, ALU.mult, None, ["smx0"], ["smx1"])
        act(Pf[0:64, :], bank(bsc)[0:64, 0:256], AF.Exp, bkeys(bsc) + ["smx1"], ["Pf", "smx2"], bias=smx[0:64, 1:2],
            accum_out=smx[0:64, 2:3])
        S.op("dve", lambda e: e.reciprocal(out=smx[0:64, 3:4], in_=smx[0:64, 2:3]), ["smx2"], ["smx3"])
        bi = nextbank()
        for mt in range(2):
            tr(bank(bi)[:, mt * 64:(mt + 1) * 64], Pf[0:64, mt * 128:(mt + 1) * 128], ident_f[0:64, 0:64], ["Pf", "ident_f"],
               bkeys(bi, 0, 1))
        for mt in range(2):
            tt("dve", PTm[:, mt, :].rearrange("p (s q) -> p s q", s=16),
               bc(bank(bi)[:, mt * 64:(mt + 1) * 64].unsqueeze(1), [128, 16, 64]),
               selm.rearrange("p (s q) -> p s q", s=16), ALU.mult, bkeys(bi, 0, 1) + ["selm"], ["PTm"])
        cnt = 0
        for s in range(TS):
            vb = Vs[s % 4]
            vk = "Vs%d" % (s % 4)
            if s >= 1 and s + 3 < TS:
                v_load(s + 3)
            for mt in range(2):
                mm(bank(bsc)[0:64, :], PTm[:, mt, s * 64:(s + 1) * 64], vb[:, mt, :], cnt == 0, cnt == 31, ["PTm", vk], bkeys(bsc))
                cnt += 1
        ts("dve", opm[0:64, :], bank(bsc)[0:64, 0:128], hm[0:64, 0:1], None, ALU.mult, None, bkeys(bsc) + ["hm"], ["opm"])
        for h2 in range(1, 4):
            stt(opm[0:64, :], bank(bsc)[0:64, h2 * 128:(h2 + 1) * 128], hm[0:64, h2:h2 + 1], opm[0:64, :], ALU.mult, ALU.add,
                bkeys(bsc) + ["hm", "opm"], ["opm"])
        ts("dve", opm[0:64, :], opm[0:64, :], smx[0:64, 3:4], None, ALU.mult, None, ["opm", "smx3"], ["opm"])
        bi = nextbank()
        tr(bank(bi)[:, 0:64], opm[0:64, :], ident_f[0:64, 0:64], ["opm", "ident_f"], bkeys(bi, 0, 1))
        tt("dve", ycT[:, :, 2048:2064], bank(bi)[:, 0:64].rearrange("p (h s) -> p h s", h=4), cgT[:, :, 2048:2064], ALU.mult,
           bkeys(bi, 0, 1) + ["cgT"], ["ycT_s"])

        if STOP <= 5:
            S.emit()
            return nc
        S.barrier()
        A.off = mark0
        A.top = TOP_YB
        wB = [A.bf16(8, 512) for _ in range(3)]
        vn_bf = A.bf16(16, 512)
        ug = A.bf16(4, 2048)
        lngB = A.f32(512)
        lnbB = A.f32(512)
        ws_f = A.f32(4, 128)
        ws_b = A.bf16(4, 128)
        wsT_b = A.bf16(4, 128)
        bsp = A.f32(512)
        tln = [A.f32(512), A.f32(512)]
        sgt = [A.bf16(512), A.bf16(512)]
        bst = A.f32(20, 8)
        vn_s = A.f32(512)
        bus = A.f32(512)
        bgs = A.f32(512)
        ws00 = A.f32(4)
        b0 = A.f32(4)
        load_w(wB[0], w_in_d[:, OFF["bv"]:OFF["bv"] + 512], "wB0")
        load_w(wB[1], w_in_d[:, OFF["bg"]:OFF["bg"] + 512], "wB1")
        load_w(wB[2], w_in_d[:, OFF["bu"]:OFF["bu"] + 512], "wB2")
        S.dma("sp", lngB, ln_v_g_d.partition_broadcast(128), writes=["lngB"])
        S.dma("sp", lnbB, ln_v_b_d.partition_broadcast(128), writes=["lnbB"])
        S.dma("act", ws_f, w_sp_d.rearrange("g t s -> t g s"), writes=["ws_f"])
        S.dma("sp", bsp[0:1, :], b_sp_d.rearrange("g t -> (g t)").partition_broadcast(1), writes=["bsp"])
        S.dma("sp", ws00[0:TS, :], bass.AP(tensor=w_sp_d.tensor, offset=w_sp_d.offset, ap=[[0, TS], [128 * 128, 4]]),
              writes=["ws00"], allow_slow_non_contiguous=True)
        S.dma("sp", b0[0:TS, :], bass.AP(tensor=b_sp_d.tensor, offset=b_sp_d.offset, ap=[[0, TS], [128, 4]]),
              writes=["b0"], allow_slow_non_contiguous=True)
        for g_ in range(4):
            asel(ws_f[:, g_, :], ws_f[:, g_, :], [[-1, 128]], ALU.is_ge, 0.0, 0, 1, ["ws_f"], ["ws_f"])
        cp("dve", ws_b, ws_f, ["ws_f"], ["ws_b"])
        bi = nextbank()
        pT = bank(bi).bitcast(BF16)
        for g_ in range(4):
            tr(pT[:, g_ * 128:(g_ + 1) * 128], ws_b[:, g_, :], ident_b, ["ws_b", "ident_b"], bkeys(bi))
        cp("dve", wsT_b.rearrange("p g t -> p (g t)"), pT[:, 0:512], bkeys(bi), ["wsT_b"])

        def ln_A(rows, src_ps, rk, idx):
            S.op("dve", lambda e: e.bn_stats(out=bst[0:rows, idx, 0:6], in_=src_ps), rk, ["bst%d" % idx])
            S.op("dve", lambda e: e.bn_aggr(out=bst[0:rows, idx, 6:8], in_=bst[0:rows, idx, 0:6]), ["bst%d" % idx], ["bag%d" % idx])
            rsqrt_col(bst[0:rows, idx, 0:1], bst[0:rows, idx, 7:8], rows, 1.0, ["bag%d" % idx], ["brs%d" % idx])

        def ln_B(rows, src_ps, rk, idx, out_ap, okey):
            t_ = tln[idx % 2][0:rows, :]
            tk = "tln%d" % (idx % 2)
            ts("dve", t_, src_ps, bst[0:rows, idx, 6:7], bst[0:rows, idx, 0:1], ALU.subtract, ALU.mult,
               rk + ["bag%d" % idx, "brs%d" % idx], [tk])
            tt("pool", t_, t_, lngB[0:rows, :], ALU.mult, [tk, "lngB"], [tk])
            tt("pool", out_ap, t_, lnbB[0:rows, :], ALU.add, [tk, "lnbB"], [okey])

        prevb = None
        for n in range(NT + 1):
            bi = nextbank()
            if n < NT:
                proj_tm(wB[0], 0, 512, n * 128, (n + 1) * 128, bank(bi), ["wB0", "hT%d" % n], bkeys(bi))
                cur = (128, bank(bi), bkeys(bi), n, vn_bf[:, n, :], "vn%d" % n)
            else:
                proj_tm(wB[0], 0, 512, 2048, 2064, bank(bi)[0:TS, :], ["wB0"] + HS, bkeys(bi))
                cur = (TS, bank(bi)[0:TS, :], bkeys(bi), 16, vn_s[0:TS, :], "vn_s")
            ln_A(*cur[:4])
            if prevb is not None:
                ln_B(*prevb)
            prevb = cur
        ln_B(*prevb)
        S.dma("sp", cvs_d, vn_s[0:TS, :], reads=["vn_s"])
        for c in range(4):
            for tb in range(4):
                tsl = slice(tb * 512, (tb + 1) * 512)
                b1 = nextbank()
                proj_fm(wB[1], c * 128, (c + 1) * 128, tb * 512, (tb + 1) * 512, bank(b1), ["wB1"] + hkeys(tb), bkeys(b1))
                sg_ = sgt[(c * 4 + tb) % 2]
                sk = "sgt%d" % ((c * 4 + tb) % 2)
                act(sg_, bank(b1), AF.Silu, bkeys(b1), [sk])
                b2 = nextbank()
                proj_fm(wB[2], c * 128, (c + 1) * 128, tb * 512, (tb + 1) * 512, bank(b2), ["wB2"] + hkeys(tb), bkeys(b2))
                tt("dve", ug[:, c, tsl], bank(b2), sg_, ALU.mult, bkeys(b2) + [sk], ["ug"])
        b1 = nextbank()
        proj_tm(wB[1], 0, 512, 2048, 2064, bank(b1)[0:TS, :], ["wB1"] + HS, bkeys(b1))
        act(bgs[0:TS, :], bank(b1)[0:TS, :], AF.Silu, bkeys(b1), ["bgs"])
        b2 = nextbank()
        proj_tm(wB[2], 0, 512, 2048, 2064, bank(b2)[0:TS, :], ["wB2"] + HS, bkeys(b2))
        tt("dve", bus[0:TS, :], bank(b2)[0:TS, :], bgs[0:TS, :], ALU.mult, bkeys(b2) + ["bgs"], ["bus"])
        for n in range(NT):
            bi = nextbank()
            for g_ in range(4):
                o_ = bank(bi)[:, g_ * 128:(g_ + 1) * 128]
                mm(o_, vn_bf[:, n, g_ * 128:(g_ + 1) * 128], wsT_b[:, g_, :], True, False, ["vn%d" % n, "wsT_b"], bkeys(bi))
                mm(o_, ones_f[0:1, :], bsp[0:1, g_ * 128:(g_ + 1) * 128], False, True, ["ones_f", "bsp"], bkeys(bi))
            tt("dve", ybT[:, :, n * 128:(n + 1) * 128], bank(bi).rearrange("p (g t) -> p g t", g=4),
               ug[:, :, n * 128:(n + 1) * 128], ALU.mult, bkeys(bi) + ["ug"], ["ybT"])
        v4 = lambda ap: ap[0:TS, :].rearrange("p (g c) -> p g c", g=4)
        tt("dve", v4(vn_s), v4(vn_s), bc(ws00[0:TS, :].unsqueeze(2), [TS, 4, 128]), ALU.mult, ["vn_s", "ws00"], ["vn_s"])
        tt("dve", v4(vn_s), v4(vn_s), bc(b0[0:TS, :].unsqueeze(2), [TS, 4, 128]), ALU.add, ["vn_s", "b0"], ["vn_s"])
        tt("dve", bus[0:TS, :], bus[0:TS, :], vn_s[0:TS, :], ALU.mult, ["bus", "vn_s"], ["bus"])
        bi = nextbank()
        for g_ in range(4):
            tr(bank(bi)[:, g_ * 16:(g_ + 1) * 16], bus[0:TS, g_ * 128:(g_ + 1) * 128], ident_f[0:TS, 0:TS], ["bus", "ident_f"],
               bkeys(bi, 0, 1))
        cp("dve", ybT[:, :, 2048:2064], bank(bi)[:, 0:64].rearrange("p (g s) -> p g s", g=4), bkeys(bi, 0, 1), ["ybT_s"])

        if STOP <= 6:
            S.emit()
            return nc
        S.barrier()
        A.off = mark0
        A.top = TOP_YB
        mT = A.bf16(8, 2064)
        wo = A.bf16(8, 1024)
        fgB = A.f32(1024)
        markM1 = A.off
        wg = [[A.bf16(8, 128) for _ in range(3)] for _ in range(2)]
        wbr = [[A.bf16(4, 128) for _ in range(3)] for _ in range(2)]
        gsb = [[A.f32(512) for _ in range(3)] for _ in range(2)]
        t0b = [A.f32(512) for _ in range(2)]
        t1b = [A.f32(512) for _ in range(2)]
        t2b = [A.f32(512) for _ in range(2)]
        yT = [yaT, ybT, ycT]
        ykeys = [["yaT", "yaT_s"], ["ybT", "ybT_s"], ["ycT", "ycT_s"]]
        wbr_d = [w_bra_d, w_brb_d, w_brc_d]
        it = 0
        def load_dc(dc_):
            w2 = dc_ % 2
            for i in range(3):
                load_w(wg[w2][i], w_in_d[:, OFF["mg"] + i * 1024 + dc_ * 128: OFF["mg"] + i * 1024 + (dc_ + 1) * 128], "wg%d_%d" % (w2, i))
                load_w(wbr[w2][i], wbr_d[i][:, dc_ * 128:(dc_ + 1) * 128], "wbr%d_%d" % (w2, i))

        load_dc(0)
        for dc in range(8):
            ws_ = dc % 2
            if dc + 1 < 8:
                load_dc(dc + 1)
            if dc == 1:
                load_w(wo[:, :, 0:512], w_out_d[:, 0:512], "wo0")
                load_w(wo[:, :, 512:1024], w_out_d[:, 512:1024], "wo1")
                S.dma("sp", fgB, fng_d.partition_broadcast(128), writes=["fgB"])
            for (t0, t1, hk) in TBL:
                w_ = t1 - t0
                pp = it % 2
                it += 1
                for i in range(3):
                    bi = nextbank()
                    proj_fm(wg[ws_][i], 0, 128, t0, t1, bank(bi)[:, 0:w_], ["wg%d_%d" % (ws_, i)] + hk, bkeys(bi))
                    act(gsb[pp][i][:, 0:w_], bank(bi)[:, 0:w_], AF.Sigmoid, bkeys(bi) + ["bgT"], ["gsb%d_%d" % (pp, i)],
                        bias=bgT[:, i, dc:dc + 1])
                tmp = [t0b[pp], t1b[pp], t2b[pp]]
                tk = ["t0b%d" % pp, "t1b%d" % pp, "t2b%d" % pp]
                for i in range(3):
                    bi = nextbank()
                    proj_fm(wbr[ws_][i], 0, 128, t0, t1, bank(bi)[:, 0:w_], ["wbr%d_%d" % (ws_, i)] + ykeys[i], bkeys(bi), K=4, src=yT[i])
                    tt("dve", tmp[i][:, 0:w_], bank(bi)[:, 0:w_], gsb[pp][i][:, 0:w_], ALU.mult, bkeys(bi) + ["gsb%d_%d" % (pp, i)], [tk[i]])
                tt("pool", tmp[0][:, 0:w_], tmp[0][:, 0:w_], tmp[1][:, 0:w_], ALU.add, [tk[0], tk[1]], [tk[0]])
                tt("pool", mT[:, dc, t0:t1], tmp[0][:, 0:w_], tmp[2][:, 0:w_], ALU.add, [tk[0], tk[2]], ["mT%d" % dc])
        MK = ["mT%d" % dc for dc in range(8)]
        if STOP <= 7:
            S.emit()
            return nc
        S.barrier()
        A.off = markM1
        A.top = AW
        xo = [A.f32(1024), A.f32(1024)]
        oo = [A.f32(1024), A.f32(1024)]
        junk2 = A.bf16(1024)
        ost = A.f32(20, 4)
        for n in range(NT + 1):
            rows = 128 if n < NT else TS
            t0 = n * 128
            b = n % 2
            src = x_d[n * 128:(n + 1) * 128, :] if n < NT else xs_d
            dst = y_d[n * 128:(n + 1) * 128, :] if n < NT else ys_d
            S.dma("act", xo[b][0:rows, :], src, writes=["xo%d" % b])
            for half in range(2):
                bi = nextbank()
                proj_tm(wo, half * 512, (half + 1) * 512, t0, t0 + rows, bank(bi)[0:rows, :], ["wo%d" % half] + MK, bkeys(bi), src=mT)
                tt("dve", oo[b][0:rows, half * 512:(half + 1) * 512], bank(bi)[0:rows, :], xo[b][0:rows, half * 512:(half + 1) * 512],
                   ALU.add, bkeys(bi) + ["xo%d" % b], ["oo%d_%d" % (b, half)])
            ok = ["oo%d_0" % b, "oo%d_1" % b]
            act(junk2[0:rows, :], oo[b][0:rows, :], AF.Square, ok, ["junk2", "ost%d" % n], accum_out=ost[0:rows, n, 0:1])
            rsqrt_col(ost[0:rows, n, 1:2], ost[0:rows, n, 0:1], rows, 1.0 / 1024, ["ost%d" % n], ["ostr%d" % n])
            stt(oo[b][0:rows, :], oo[b][0:rows, :], ost[0:rows, n, 1:2], fgB[0:rows, :], ALU.mult, ALU.mult,
                ok + ["ostr%d" % n, "fgB"], ["oof%d" % b] + ok)
            S.dma("sp", dst, oo[b][0:rows, :], reads=["oof%d" % b] + ok)
        S.emit()
    return nc


_CACHE = {}


def kernel(x_prompt, x_sample, cache_mem_k, cache_mem_v, state_delta, state_conv, mem_prompt,
           norm_g, w_in, conv_w, a_log, dt_bias, a_norm_g, ln_v_g, ln_v_b, w_spatial, b_spatial,
           mem_norm_g, w_mem_kv, w_br_a, w_br_b, w_br_c, b_gate, w_out, final_norm_g):
    f = lambda a: np.ascontiguousarray(np.asarray(a, dtype=np.float32))
    n = 8
    if "nc" not in _CACHE:
        _CACHE["nc"] = build()
    nc = _CACHE["nc"]
    shared = dict(norm_g=f(norm_g)[0], w_in=f(w_in)[0], conv_w=f(conv_w)[0], a_log=f(a_log)[0], dt_bias=f(dt_bias)[0],
                  a_norm_g=f(a_norm_g)[0], ln_v_g=f(ln_v_g)[0], ln_v_b=f(ln_v_b)[0], w_spatial=f(w_spatial)[0],
                  b_spatial=f(b_spatial)[0], mem_norm_g=f(mem_norm_g)[0], w_mem_kv=f(w_mem_kv)[0], w_br_a=f(w_br_a)[0],
                  w_br_b=f(w_br_b)[0], w_br_c=f(w_br_c)[0], b_gate=f(b_gate)[0], w_out=f(w_out)[0],
                  final_norm_g=f(final_norm_g))
    xp, xs = f(x_prompt), f(x_sample)
    cmk, cmv, sd, sc, mem = f(cache_mem_k), f(cache_mem_v), f(state_delta), f(state_conv), f(mem_prompt)
    in_maps = []
    for b in range(n):
        sl = slice(16 * b, 16 * b + 16)
        m = dict(shared)
        m.update(x=xp[b], xs=xs[sl, 0, :], cmk=cmk[0, sl].reshape(16, 256, 512), cmv=cmv[0, sl].reshape(16, 256, 512),
                 sd=sd[0, sl], sc=sc[0, sl], mem=mem[b])
        in_maps.append(m)
    res = run_bass_kernel_spmd(nc, in_maps, core_ids=list(range(n)))
    R = res.results
    y_prompt = np.stack([R[b]["y"] for b in range(n)], 0)
    y_sample = np.concatenate([R[b]["ys"] for b in range(n)], 0)[:, None, :]
    sdp = np.stack([R[b]["sdp"] for b in range(n)], 0)[None]
    scp = np.stack([R[b]["scp"] for b in range(n)], 0)[None]
    mkp = np.stack([R[b]["mkp"].reshape(256, 4, 128) for b in range(n)], 0)[None]
    mvp = np.stack([R[b]["mvp"].reshape(256, 4, 128) for b in range(n)], 0)[None]
    sds = np.concatenate([R[b]["sds"] for b in range(n)], 0)[None]
    scs = np.concatenate([R[b]["scs"] for b in range(n)], 0)[None]
    cvs = np.concatenate([R[b]["cvs"] for b in range(n)], 0)[None, :, None, :]
    return (y_prompt.astype(np.float32), y_sample.astype(np.float32), sdp.astype(np.float32), scp.astype(np.float32),
            mkp.astype(np.float32), mvp.astype(np.float32), sds.astype(np.float32), scs.astype(np.float32),
            cvs.astype(np.float32))
```
